# Optimizing a Trainium2 kernel written in Bass

```python
import jax, jax.numpy as jnp
from jax import lax
import numpy as np

D_MODEL = 1024
BATCH = 4
SEQ = 8192
DEPTH = 1
DEC_BATCH = 128
DEC_SEQ = 4
PAST_LEN = 16384
PAGE_SIZE = 128

HEAD_DIM = 64
N_HEADS = D_MODEL // HEAD_DIM
N_CHUNK_HEADS = N_HEADS // 2
N_ATTN_HEADS = N_HEADS - N_CHUNK_HEADS
N_KV_HEADS = 2
Q_PER_KV = N_ATTN_HEADS // N_KV_HEADS
CHUNK_WIDTH = N_CHUNK_HEADS * HEAD_DIM
ATTN_WIDTH = N_ATTN_HEADS * HEAD_DIM
KV_WIDTH = N_KV_HEADS * HEAD_DIM
IN_WIDTH = 2 * CHUNK_WIDTH + ATTN_WIDTH + 2 * KV_WIDTH
SPLITS = (CHUNK_WIDTH, 2 * CHUNK_WIDTH, 2 * CHUNK_WIDTH + ATTN_WIDTH, 2 * CHUNK_WIDTH + ATTN_WIDTH + KV_WIDTH)
CHUNK = 128
WINDOW = 128
D_FF = 4 * D_MODEL
EPS = 1e-6
ATTN_SCALE = HEAD_DIM ** -0.5

kernel_name = 'hymba_chunkmlp_swa_sink_step'


def rms_norm(x, g):
    xf = x.astype(jnp.float32)
    y = xf * lax.rsqrt(jnp.mean(xf * xf, axis=-1, keepdims=True) + EPS)
    return (y * g.astype(jnp.float32)).astype(x.dtype)


def layer_norm(x, g, b):
    xf = x.astype(jnp.float32)
    xc = xf - jnp.mean(xf, axis=-1, keepdims=True)
    y = xc * lax.rsqrt(jnp.mean(xc * xc, axis=-1, keepdims=True) + EPS)
    return (y * g.astype(jnp.float32) + b.astype(jnp.float32)).astype(x.dtype)


def alibi_slopes():
    i = jnp.arange(1, N_ATTN_HEADS + 1, dtype=jnp.float32)
    return jnp.exp2(-8.0 * i / N_ATTN_HEADS).reshape(N_KV_HEADS, Q_PER_KV)


def mixer_projections(x, g_pre, w_in, ln_g, ln_b):
    h = rms_norm(x, g_pre)
    z = jnp.einsum('bsd,de->bse', h, w_in)
    u, v, q, k, val = jnp.split(z, SPLITS, axis=-1)
    u = jax.nn.gelu(u)
    v = layer_norm(jax.nn.gelu(v), ln_g, ln_b)
    return u, v, q, k, val


def spatial_gate(u, v, w_s, b_s):
    B, C, n, _ = v.shape
    causal = jnp.tril(jnp.ones((n, n), dtype=bool))
    w = jnp.where(causal, w_s[:, :n, :n], 0).astype(v.dtype)
    vh = v.reshape(B, C, n, N_CHUNK_HEADS, HEAD_DIM)
    s = jnp.einsum('gts,bcsgd->bctgd', w, vh) + b_s[:, :n].T[:, :, None].astype(v.dtype)
    return u * s.reshape(B, C, n, CHUNK_WIDTH)


def sink_attention(q, k, v, dist, valid, sinks):
    s = jnp.einsum('...qhrd,...khd->...hrqk', q, k, preferred_element_type=jnp.float32) * ATTN_SCALE
    s = s - alibi_slopes()[:, :, None, None] * dist.astype(jnp.float32)
    s = jnp.where(valid, s, -jnp.inf)
    sink = sinks.astype(jnp.float32)[:, :, None, None]
    m = jnp.maximum(jnp.max(s, axis=-1, keepdims=True), sink)
    p = jnp.exp(s - m)
    z = jnp.sum(p, axis=-1, keepdims=True) + jnp.exp(sink - m)
    p = (p / z).astype(v.dtype)
    return jnp.einsum('...hrqk,...khd->...qhrd', p, v)


def swa_prompt(q, k, v, sinks):
    B, S, _ = q.shape
    nb = S // WINDOW
    qb = q.reshape(B, nb, WINDOW, N_KV_HEADS, Q_PER_KV, HEAD_DIM)

    def band(t):
        t = t.reshape(B, S, N_KV_HEADS, HEAD_DIM)
        t = jnp.pad(t, ((0, 0), (WINDOW, 0), (0, 0), (0, 0)))
        t = t.reshape(B, nb + 1, WINDOW, N_KV_HEADS, HEAD_DIM)
        return jnp.concatenate([t[:, :-1], t[:, 1:]], axis=2)

    kb, vb = band(k), band(v)
    a = jnp.arange(WINDOW)[:, None]
    c = jnp.arange(2 * WINDOW)[None, :]
    dist = WINDOW + a - c
    kpos = (jnp.arange(nb)[:, None, None] - 1) * WINDOW + c[None]
    valid = (dist >= 0) & (dist <= WINDOW) & (kpos >= 0)
    out = sink_attention(qb, kb, vb, dist, valid[:, None, None], sinks)
    return out.reshape(B, S, ATTN_WIDTH)


def swa_sample(q, k, v, cache_k, cache_v, sinks):
    Bd, L, _ = q.shape
    Wc = cache_k.shape[1]
    qh = q.reshape(Bd, L, N_KV_HEADS, Q_PER_KV, HEAD_DIM)
    kc = jnp.concatenate([cache_k, k.reshape(Bd, L, N_KV_HEADS, HEAD_DIM)], axis=1)
    vc = jnp.concatenate([cache_v, v.reshape(Bd, L, N_KV_HEADS, HEAD_DIM)], axis=1)
    dist = (Wc + jnp.arange(L)[:, None]) - jnp.arange(Wc + L)[None, :]
    valid = (dist >= 0) & (dist <= WINDOW)
    out = sink_attention(qh, kc, vc, dist, valid, sinks)
    return out.reshape(Bd, L, ATTN_WIDTH), kc[:, L:], vc[:, L:]


def merge_and_channel_mix(x, a_out, b_out, g_out_chunk, g_out_attn, w_o, g_post_mix,
                          g_pre_ffn, w_up, w_down, g_post_ffn):
    merged = jnp.concatenate([rms_norm(a_out, g_out_chunk), rms_norm(b_out, g_out_attn)], axis=-1)
    o = jnp.einsum('bsc,cd->bsd', merged, w_o)
    x = x + rms_norm(o, g_post_mix)
    h = rms_norm(x, g_pre_ffn)
    f = jnp.einsum('bsf,fd->bsd', jnp.square(jax.nn.relu(jnp.einsum('bsd,df->bsf', h, w_up))), w_down)
    return x + rms_norm(f, g_post_ffn)


def setup_inputs(seed: int = 0) -> dict:
    key = jax.random.key(seed)
    ks = jax.random.split(key, 20)
    f32 = jnp.float32
    win = min(WINDOW, PAST_LEN)
    nrm = lambda k, shape: jax.random.normal(k, shape, f32)
    return {
        'x_prompt': nrm(ks[0], (BATCH, SEQ, D_MODEL)),
        'x_sample': nrm(ks[1], (DEC_BATCH, DEC_SEQ, D_MODEL)),
        'cache_win_k': nrm(ks[2], (DEPTH, DEC_BATCH, win, N_KV_HEADS, HEAD_DIM)),
        'cache_win_v': nrm(ks[3], (DEPTH, DEC_BATCH, win, N_KV_HEADS, HEAD_DIM)),
        'w_in': nrm(ks[4], (DEPTH, D_MODEL, IN_WIDTH)) * D_MODEL ** -0.5,
        'g_pre_mix': 1.0 + 0.05 * nrm(ks[5], (DEPTH, D_MODEL)),
        'ln_v_g': 1.0 + 0.05 * nrm(ks[6], (DEPTH, CHUNK_WIDTH)),
        'ln_v_b': 0.02 * nrm(ks[7], (DEPTH, CHUNK_WIDTH)),
        'w_spatial': nrm(ks[8], (DEPTH, N_CHUNK_HEADS, CHUNK, CHUNK)) * 0.5 * CHUNK ** -0.5,
        'b_spatial': 1.0 + 0.1 * nrm(ks[9], (DEPTH, N_CHUNK_HEADS, CHUNK)),
        'attn_sinks': 0.5 * nrm(ks[10], (DEPTH, N_KV_HEADS, Q_PER_KV)),
        'g_out_chunk': 1.0 + 0.05 * nrm(ks[11], (DEPTH, CHUNK_WIDTH)),
        'g_out_attn': 1.0 + 0.05 * nrm(ks[12], (DEPTH, ATTN_WIDTH)),
        'w_o': nrm(ks[13], (DEPTH, CHUNK_WIDTH + ATTN_WIDTH, D_MODEL)) * (CHUNK_WIDTH + ATTN_WIDTH) ** -0.5,
        'g_post_mix': 1.0 + 0.05 * nrm(ks[14], (DEPTH, D_MODEL)),
        'g_pre_ffn': 1.0 + 0.05 * nrm(ks[15], (DEPTH, D_MODEL)),
        'w_up': nrm(ks[16], (DEPTH, D_MODEL, D_FF)) * D_MODEL ** -0.5,
        'w_down': nrm(ks[17], (DEPTH, D_FF, D_MODEL)) * D_FF ** -0.5,
        'g_post_ffn': 1.0 + 0.05 * nrm(ks[18], (DEPTH, D_MODEL)),
    }


def reference(x_prompt, x_sample, cache_win_k, cache_win_v, w_in, g_pre_mix, ln_v_g, ln_v_b,
              w_spatial, b_spatial, attn_sinks, g_out_chunk, g_out_attn, w_o, g_post_mix,
              g_pre_ffn, w_up, w_down, g_post_ffn):
    yp, ys = x_prompt, x_sample
    B, S, _ = x_prompt.shape
    Bd, L, _ = x_sample.shape
    wk_p, wv_p, cv_p, wk_s, wv_s, cv_s = [], [], [], [], [], []
    for l in range(DEPTH):
        u, v, q, k, val = mixer_projections(yp, g_pre_mix[l], w_in[l], ln_v_g[l], ln_v_b[l])
        a_out = spatial_gate(u.reshape(B, S // CHUNK, CHUNK, CHUNK_WIDTH),
                             v.reshape(B, S // CHUNK, CHUNK, CHUNK_WIDTH),
                             w_spatial[l], b_spatial[l]).reshape(B, S, CHUNK_WIDTH)
        b_out = swa_prompt(q, k, val, attn_sinks[l])
        yp = merge_and_channel_mix(yp, a_out, b_out, g_out_chunk[l], g_out_attn[l], w_o[l],
                                   g_post_mix[l], g_pre_ffn[l], w_up[l], w_down[l], g_post_ffn[l])
        wk_p.append(k.reshape(B, S, N_KV_HEADS, HEAD_DIM)[:, S - WINDOW:])
        wv_p.append(val.reshape(B, S, N_KV_HEADS, HEAD_DIM)[:, S - WINDOW:])
        cv_p.append(v[:, S - CHUNK:])

        u, v, q, k, val = mixer_projections(ys, g_pre_mix[l], w_in[l], ln_v_g[l], ln_v_b[l])
        a_out = spatial_gate(u[:, None], v[:, None], w_spatial[l], b_spatial[l])[:, 0]
        b_out, new_k, new_v = swa_sample(q, k, val, cache_win_k[l], cache_win_v[l], attn_sinks[l])
        ys = merge_and_channel_mix(ys, a_out, b_out, g_out_chunk[l], g_out_attn[l], w_o[l],
                                   g_post_mix[l], g_pre_ffn[l], w_up[l], w_down[l], g_post_ffn[l])
        wk_s.append(new_k)
        wv_s.append(new_v)
        cv_s.append(v)
    return (yp, ys, jnp.stack(wk_p), jnp.stack(wv_p), jnp.stack(cv_p),
            jnp.stack(wk_s), jnp.stack(wv_s), jnp.stack(cv_s))
```

```python
from contextlib import ExitStack

import numpy as np
import concourse.bass as bass
import concourse.mybir as mybir
from concourse.bass_utils import run_bass_kernel_spmd

F32 = mybir.dt.float32
BF16 = mybir.dt.bfloat16
AF = mybir.ActivationFunctionType
ALU = mybir.AluOpType

NCORES = 8
D = 1024
TOK_CORE = 4096
NSB = 8
SEQ_CORE = 16
EPS = 1e-6
NEG = -80000.0
RING_M = 3
RING_F = 4


class Sem:
    def __init__(self, nc, es, name):
        self.h = es.enter_context(nc.semaphore(name))
        self.total = 0


class Eng:
    def __init__(self, nc, es, eng, name, is_pe=False):
        self.e = eng
        self.sem = Sem(nc, es, "c_" + name)
        self.waited = {}
        self.is_pe = is_pe


class Buf:
    __slots__ = ("w", "r", "excl")

    def __init__(self, excl=False):
        self.w = None
        self.r = {}
        self.excl = excl


def _deps(E, reads, writes):
    deps = {}

    def add(s, v):
        if deps.get(s, 0) < v:
            deps[s] = v

    for b in reads:
        if b.w is not None:
            add(*b.w)
    for b in writes:
        if b.w is not None:
            add(*b.w)
        for s, v in b.r.items():
            add(s, v)
    for s, v in deps.items():
        if s is E.sem and E.is_pe:
            continue
        if E.waited.get(s, 0) >= v:
            continue
        E.e.wait_ge(s.h, v)
        E.waited[s] = v


def _reg(ev, reads, writes):
    for b in reads:
        if b.r.get(ev[0], 0) < ev[1]:
            b.r[ev[0]] = ev[1]
    for b in writes:
        b.w = ev
        b.r = {}


def op(E, fn, reads=(), writes=(), signal=True):
    if any(b.excl for b in reads):
        writes = list(writes) + [b for b in reads if b.excl]
        reads = [b for b in reads if not b.excl]
    _deps(E, reads, writes)
    inst = fn()
    if signal:
        E.sem.total += 1
        inst.then_inc(E.sem.h, 1)
        ev = (E.sem, E.sem.total)
    else:
        ev = (E.sem, E.sem.total + 1)
    _reg(ev, reads, writes)


def dma(Q, sem, out, in_, reads=(), writes=(), **kw):
    _deps(Q, reads, writes)
    inst = Q.e.dma_start(out=out, in_=in_, **kw)
    sem.total += 16
    inst.then_inc(sem.h, 16)
    _reg((sem, sem.total), reads, writes)


def build_program():
    nc = bass.Bass("TRN2", target_bir_lowering=False)
    es = ExitStack()
    _build(nc, es)
    return nc, es


STEP_COUNTS = []


def _drive(main, side=None, ratio=1.0, delay=0):
    nm = ns = 0
    acc = 0.0
    main_done = main is None
    side_done = side is None
    while not (main_done and side_done):
        if not main_done:
            try:
                next(main)
                nm += 1
            except StopIteration:
                main_done = True
        if not side_done and (nm > delay or main_done):
            acc += ratio if not main_done else 1e9
            while acc >= 1.0 and not side_done:
                acc -= 1.0
                try:
                    next(side)
                    ns += 1
                except StopIteration:
                    side_done = True
    STEP_COUNTS.append((nm, ns))


def _build(nc, es):
    def dram(name, shape, dt=F32, kind="ExternalInput"):
        return nc.dram_tensor(name, shape, dt, kind=kind).ap()

    def sb(name, shape, dt=F32):
        return es.enter_context(nc.sbuf_tensor(name, shape, dt))

    def psum(name, shape, dt=F32):
        return es.enter_context(nc.psum_tensor(name, shape, dt))

    xp = dram("xp", [TOK_CORE, D])
    xh = dram("xh", [128, D])
    flag_d = dram("flag", [128, 1])
    xs = dram("xs", [64, D])
    ck = dram("ck", [SEQ_CORE, 128, 128])
    cv = dram("cv", [SEQ_CORE, 128, 128])
    w_in = dram("w_in", [D, 1792])
    g_pre_mix = dram("g_pre_mix", [1, D])
    ln_v_g = dram("ln_v_g", [1, 512])
    ln_v_b = dram("ln_v_b", [1, 512])
    w_spatial = dram("w_spatial", [8, 128, 128])
    b_spatial = dram("b_spatial", [8, 128])
    attn_sinks = dram("attn_sinks", [1, 8])
    g_out_chunk = dram("g_out_chunk", [1, 512])
    g_out_attn = dram("g_out_attn", [1, 512])
    w_o = dram("w_o", [D, D])
    g_post_mix = dram("g_post_mix", [1, D])
    g_pre_ffn = dram("g_pre_ffn", [1, D])
    w_up = dram("w_up", [D, 4096])
    w_down = dram("w_down", [4096, D])
    g_post_ffn = dram("g_post_ffn", [1, D])

    yp = dram("yp", [TOK_CORE, D], kind="ExternalOutput")
    ys = dram("ys", [64, D], kind="ExternalOutput")
    wkp = dram("wkp", [128, 128], kind="ExternalOutput")
    wvp = dram("wvp", [128, 128], kind="ExternalOutput")
    cvp = dram("cvp", [128, 512], kind="ExternalOutput")
    wks = dram("wks", [SEQ_CORE, 128, 128], kind="ExternalOutput")
    wvs = dram("wvs", [SEQ_CORE, 128, 128], kind="ExternalOutput")
    cvs = dram("cvs", [64, 512], kind="ExternalOutput")

    NPIECE = 22
    scratch = dram("wscratch", [NPIECE, 128, 8, 512], BF16, kind="Internal")

    PE = Eng(nc, es, nc.tensor, "pe", is_pe=True)
    ACT = Eng(nc, es, nc.scalar, "act")
    DVE = Eng(nc, es, nc.vector, "dve")
    POOL = Eng(nc, es, nc.gpsimd, "pool")
    SP = Eng(nc, es, nc.sync, "sp")
    out_sems = []

    ident = sb("ident", [128, 128], BF16)
    zeros = sb("zeros", [128, 260], BF16)
    gpost_bc = sb("gpost_bc", [128, D])
    gpostffn_bc = sb("gpostffn_bc", [128, D])
    lng_bc = sb("lng_bc", [128, 512])
    lnb_bc = sb("lnb_bc", [128, 512])
    GT = sb("GT", [128, 32])
    gT_pre = GT[:, 0:8]
    gT_ffn = GT[:, 8:16]
    gT_mrg = GT[:, 16:24]
    esink = sb("esink", [128, 8])
    btok = GT[:, 24:32]
    btok_s = sb("btok_s", [64, 8])
    nhalf = sb("nhalf", [128, 1])
    flag = sb("flag_sb", [128, 1])
    wTm = sb("wTm", [128, 8, 128], BF16)
    wTs = sb("wTs", [64, 8, 64], BF16)
    Rsel = sb("Rsel", [4, 64], BF16)
    wsm = sb("wsm", [4, 32], BF16)
    biasJ = [sb("biasJ0", [128, 512], BF16), sb("biasJ1", [128, 512], BF16)]
    d4 = sb("d4", [128, 4])
    biasP = sb("biasP", [128, 2, 512], BF16)
    biasC = sb("biasC", [128, 2, 512], BF16)
    Tcache = sb("Tcache", [128, 8, 124], BF16)
    biasN = sb("biasN", [128, 8, 64], BF16)

    xbuf = [sb("xbuf0", [128, 4, D]), sb("xbuf1", [128, 4, D])]
    actT = sb("actT", [128, 8, 512], BF16)
    x2T = sb("x2T", [128, 8, 512], BF16)
    qT = sb("qT", [128, 4, 512], BF16)
    kT = sb("kT", [128, 640], BF16)
    Vaug = sb("Vaug", [128, 5, 2, 66], BF16)
    PTb = [sb("PTb0", [128, 4, 512], BF16), sb("PTb1", [128, 4, 512], BF16)]
    h2T = sb("h2T", [128, 32, 512], BF16)
    ringM_t = [sb("ringM%d" % i, [128, 8, 512], BF16) for i in range(RING_M)]
    ringF_t = [sb("ringF%d" % i, [128, 2048], BF16) for i in range(RING_F)]

    tA = [sb("tA0", [128, D], BF16), sb("tA1", [128, D], BF16)]
    tB = [sb("tB0", [128, 1536]), sb("tB1", [128, 1536])]
    v_bfs = [sb("v_bf0", [128, 512], BF16), sb("v_bf1", [128, 512], BF16)]
    fsink = sb("fsink", [128, 512], BF16)
    kvf = sb("kvf", [128, 256])
    mrg4 = sb("mrg4", [128, 4, D], BF16)
    h_bf = tA[0]
    junk = tA[1]
    t_o = tB[0][:, 512:1536]
    relu_t = [sb("relu0", [128, 512]), sb("relu1", [128, 512])]
    f0g = sb("f0g", [128, 4, 512])
    tmp_y = sb("tmp_y", [128, 512])
    stat = sb("stat", [128, 64])
    bnsts = [sb("bnst0", [128, 6]), sb("bnst1", [128, 6])]
    mvs = [sb("mv0", [128, 2]), sb("mv1", [128, 2])]
    zzs = [sb("zz0", [128, 8]), sb("zz1", [128, 8])]
    rzs = [sb("rz0", [128, 8]), sb("rz1", [128, 8])]

    wTs_f = relu_t[0][0:64, :].rearrange("p (g t) -> p g t", t=64)
    dqk = relu_t[1][:, 0:128]
    tmpf = relu_t[1][:, 128:256]
    tmpf2 = relu_t[1][:, 256:384]
    identf = relu_t[1][:, 384:512]
    biasP0 = sb("biasP0", [128, 2, 512], BF16)
    def kTc(hp, j):
        return h2T[hp, j, 128:256]

    def Vc(j):
        return h2T[:, 16 + j, 64:196].rearrange("p (h e) -> p h e", e=66)

    bank = [psum("bank%d" % i, [128, 512]) for i in range(7)]
    pT = psum("pT", [128, 8, 128], BF16)
    B_bank = [Buf(excl=True) for _ in range(7)]
    B_pT = Buf(excl=True)
    MA = [0, 1]
    M2 = 2
    FB = [3, 4, 5, 6]

    B = {}

    def bf(name):
        if name not in B:
            B[name] = Buf()
        return B[name]

    _stat_i = [0]
    stat_cols = {}

    def st(name):
        if name not in stat_cols:
            stat_cols[name] = _stat_i[0]
            _stat_i[0] += 1
            assert _stat_i[0] <= 64
        c = stat_cols[name]
        return stat[:, c:c + 1], bf("stat_" + name)

    B_x = [[Buf() for _ in range(4)] for _ in range(2)]
    S_xload = [[Sem(nc, es, "xl%d%d" % (s, b)) for b in range(4)] for s in range(2)]
    S_xstore = [[Sem(nc, es, "xs%d%d" % (s, b)) for b in range(4)] for s in range(2)]
    for s in range(2):
        out_sems.extend(S_xstore[s])
    B_actT = [Buf() for _ in range(4)]
    B_x2T = [Buf() for _ in range(4)]
    B_qT = [Buf() for _ in range(4)]
    B_kT = [Buf() for _ in range(5)]
    B_V = [Buf() for _ in range(5)]
    B_PT = [Buf(), Buf()]
    B_h2T = [Buf() for _ in range(32)]
    B_cast = [Buf() for _ in range(NPIECE)]
    S_cast = [Sem(nc, es, "cs%d" % i) for i in range(NPIECE)]
    S_setup = Sem(nc, es, "setup")
    _misc_n = [0]

    def new_out_sem():
        _misc_n[0] += 1
        sm = Sem(nc, es, "misc%d" % _misc_n[0])
        out_sems.append(sm)
        return sm

    B_setup = Buf()

    op(POOL, lambda: nc.gpsimd.memset(nhalf[:], -0.5), writes=[bf("nhalf")])
    op(POOL, lambda: nc.gpsimd.memset(zeros[:], 0.0), writes=[bf("zeros")])
    op(POOL, lambda: nc.gpsimd.memset(identf, 1.0), writes=[bf("identf")])
    op(POOL, lambda: nc.gpsimd.affine_select(out=ident[:], in_=identf, pattern=[[-1, 128]],
                                             compare_op=ALU.is_equal, fill=0.0, base=0,
                                             channel_multiplier=1),
       reads=[bf("identf")], writes=[bf("ident")])
    op(POOL, lambda: nc.gpsimd.memset(Vaug[:], 1.0), writes=B_V)
    op(POOL, lambda: nc.gpsimd.memset(Tcache[:], NEG), writes=[bf("Tcache")])
    op(POOL, lambda: nc.gpsimd.memset(biasN[:], 0.0), writes=[bf("biasN")])
    op(POOL, lambda: nc.gpsimd.iota(dqk, pattern=[[1, 128]], base=0, channel_multiplier=-1,
                                    allow_small_or_imprecise_dtypes=True), writes=[bf("dqk")])
    op(POOL, lambda: nc.gpsimd.iota(d4[:], pattern=[[1, 4]], base=128, channel_multiplier=-1,
                                    allow_small_or_imprecise_dtypes=True), writes=[bf("d4")])

    P_U, P_V, P_Q, P_KV, P_O0, P_O1 = 0, 1, 2, 3, 4, 5
    P_UP = [6 + j for j in range(8)]
    P_DN = [[14 + c * 4 + kg for kg in range(4)] for c in range(2)]
    piece_w = {P_KV: 256}

    def kpc(ap):
        return ap.rearrange("(k p) c -> p k c", p=128)

    def cast(pid, src_ap):
        w = piece_w.get(pid, 512)
        dma(POOL, S_cast[pid], scratch[pid, :, :, 0:w], src_ap, writes=[B_cast[pid]])

    cast(P_KV, kpc(w_in[:, 1536:1792]))
    cast(P_U, kpc(w_in[:, 0:512]))
    cast(P_V, kpc(w_in[:, 512:1024]))

    def casts_mixer_rest():
        for hk in range(2):
            for r in range(4):
                c0 = 1024 + hk * 256 + r * 64
                o0 = (r * 2 + hk) * 64
                dma(POOL, S_cast[P_Q], scratch[P_Q, :, :, o0:o0 + 64], kpc(w_in[:, c0:c0 + 64]),
                    writes=[B_cast[P_Q]])
        cast(P_O0, kpc(w_o[:, 0:512]))
        cast(P_O1, kpc(w_o[:, 512:1024]))

    deferred_casts = []
    for j in range(8):
        deferred_casts.append((P_UP[j], kpc(w_up[:, j * 512:(j + 1) * 512])))
    for c in range(2):
        for kg in range(4):
            deferred_casts.append((P_DN[c][kg], kpc(w_down[kg * 1024:(kg + 1) * 1024, c * 512:(c + 1) * 512])))

    class Ring:
        def __init__(self, name, tiles, seq):
            self.tiles = tiles
            self.R = len(tiles)
            self.bufs = [Buf() for _ in tiles]
            self.sems = [Sem(nc, es, "%s%d" % (name, i)) for i in range(self.R)]
            self.seq = seq
            self.next = 0
            self.cons = 0

        def load_next(self):
            n = self.next
            if n >= len(self.seq):
                return
            pid, view, src_ap = self.seq[n]
            assert B_cast[pid].w is not None, ("ring load emitted before its cast", pid)
            s = n % self.R
            dma(SP, self.sems[s], view(self.tiles[s]), src_ap, reads=[B_cast[pid]], writes=[self.bufs[s]])
            self.next = n + 1

        def open(self, pid):
            n = self.cons
            assert self.seq[n][0] == pid, (n, self.seq[n][0], pid)
            assert n < self.next
            self.cons = n + 1
            return self.seq[n][1](self.tiles[n % self.R]), self.bufs[n % self.R]

        def retire(self):
            self.load_next()

    def m_entry(pid):
        w = piece_w.get(pid, 512)
        return (pid, lambda t: t[:, :, 0:w], scratch[pid, :, :, 0:w])

    def up_entry(j, half):
        return (P_UP[j], lambda t: t[:, :].rearrange("p (k c) -> p k c", c=256),
                scratch[P_UP[j], :, :, half * 256:(half + 1) * 256])

    def dn_entry(c, kg, half):
        return (P_DN[c][kg], lambda t: t[:, :].rearrange("p (k c) -> p k c", c=512),
                scratch[P_DN[c][kg], :, half * 4:(half + 1) * 4, :])

    seqM = [m_entry(P_KV)] + [m_entry(p) for p in (P_U, P_V, P_Q, P_KV, P_O0, P_O1)] * (NSB + 1)
    seqF1 = [up_entry(j, h) for j in range(8) for h in range(2)] + \
            [dn_entry(c, kg, h) for c in range(2) for kg in range(4) for h in range(2)]
    seqF = seqF1 * (NSB + 1)
    ringM = Ring("rm", ringM_t, seqM)
    ringF = Ring("rf", ringF_t, seqF)

    def sdma(out, in_, **kw):
        dma(SP, S_setup, out, in_, writes=[B_setup], **kw)

    S_halo = Sem(nc, es, "halo")
    dma(SP, S_halo, t_o, xh[:, :], writes=[bf("tB0_1"), bf("tB0_2")])
    dma(SP, S_xload[0][0], xbuf[0][:64, 0, :], xs[:, :], writes=[B_x[0][0]])
    sdma(gpost_bc[:], g_post_mix.partition_broadcast(128))
    sdma(gpostffn_bc[:], g_post_ffn.partition_broadcast(128))
    sdma(lng_bc[:], ln_v_g.partition_broadcast(128))
    sdma(lnb_bc[:], ln_v_b.partition_broadcast(128))
    sdma(esink[:], attn_sinks.partition_broadcast(128))
    sdma(flag[:], flag_d[:, :])
    identF = tB[1][:, 0:128]
    Gst = tB[1][0:32, 128:256]
    sdma(Gst[0:8, :], g_pre_mix[0, :].rearrange("(k p) -> k p", p=128))
    sdma(Gst[8:16, :], g_pre_ffn[0, :].rearrange("(k p) -> k p", p=128))
    sdma(Gst[16:20, :], g_out_chunk[0, :].rearrange("(k p) -> k p", p=128))
    sdma(Gst[20:24, :], g_out_attn[0, :].rearrange("(k p) -> k p", p=128))
    sdma(Gst[24:32, :], b_spatial[:, :])
    w_stage = tB[1][:, 512:1536]
    dma(SP, S_setup, w_stage.rearrange("p (g s) -> p g s", s=128), w_spatial.rearrange("g t s -> t g s"),
        writes=[B_setup, bf("tB1_1"), bf("tB1_2")])
    S_bts = Sem(nc, es, "bts")
    for j in range(SEQ_CORE):
        dma(SP, S_bts, btok_s[4 * j:4 * j + 4, :], b_spatial[:, 0:4].rearrange("g t -> t g"),
            writes=[bf("btok_s")], allow_slow_non_contiguous=True)
    xb1 = xbuf[1]
    S_cache = [Sem(nc, es, "cache%d" % i) for i in range(4)]
    for half in range(2):
        dma(SP, S_cache[half], xb1[:, half, :].rearrange("p (j c) -> p j c", c=128),
            ck[half * 8:(half + 1) * 8].rearrange("j p c -> p j c"), writes=[B_x[1][half]])
        dma(SP, S_cache[2 + half], xb1[:, 2 + half, :].rearrange("p (j c) -> p j c", c=128),
            cv[half * 8:(half + 1) * 8].rearrange("j p c -> p j c"), writes=[B_x[1][2 + half]])
    for _ in range(RING_M):
        ringM.load_next()

    def setup_barrier(E):
        E.e.wait_ge(S_setup.h, S_setup.total)
        E.waited[S_setup] = S_setup.total

    for E in (POOL, DVE, ACT, PE):
        setup_barrier(E)

    op(ACT, lambda: nc.scalar.activation(out=esink[:], in_=esink[:], func=AF.Exp), writes=[bf("esink")])
    op(POOL, lambda: nc.gpsimd.affine_select(out=identF, in_=identf, pattern=[[-1, 128]],
                                             compare_op=ALU.is_equal, fill=0.0, base=0, channel_multiplier=1),
       reads=[bf("identf")], writes=[bf("tB1_0")])
    op(PE, lambda: nc.tensor.transpose(out=bank[M2][:, 0:32], in_=Gst, identity=identF[0:32, 0:32]),
       reads=[bf("tB1_0")], writes=[B_bank[M2]])
    op(DVE, lambda: nc.vector.tensor_copy(out=GT[:], in_=bank[M2][:, 0:32]), reads=[B_bank[M2]], writes=[bf("GT")])
    op(DVE, lambda: nc.vector.tensor_copy(out=h_bf[:], in_=w_stage), reads=[bf("tB1_1"), bf("tB1_2")], writes=[bf("tA0")])
    for g in range(8):
        op(PE, lambda: nc.tensor.transpose(out=pT[:, g, :], in_=h_bf[:, g * 128:(g + 1) * 128],
                                           identity=ident[:]),
           reads=[bf("tA0"), bf("ident")], writes=[B_pT], signal=(g == 7))
    op(DVE, lambda: nc.vector.tensor_copy(out=junk[:].rearrange("p (g t) -> p g t", t=128), in_=pT[:]),
       reads=[B_pT], writes=[bf("tA1")])
    op(POOL, lambda: nc.gpsimd.affine_select(out=wTm[:], in_=junk[:].rearrange("p (g t) -> p g t", t=128),
                                             pattern=[[0, 8], [1, 128]], compare_op=ALU.is_ge, fill=0.0,
                                             base=0, channel_multiplier=-1),
       reads=[bf("tA1")], writes=[bf("wTm")])
    S_TR, S_SP = 6, 10

    def slack(n):
        for _ in range(n):
            yield

    bank_lock = {}

    def acquire(b, who):
        while bank_lock.get(b) is not None:
            yield
        bank_lock[b] = who

    def release(b):
        bank_lock[b] = None

    def rstd_from(P, ss_list, n, name):
        t_ap, t_b = st(name + "_t")
        r_ap, r_b = st(name + "_r")
        src_ap, src_b = ss_list[0]
        if len(ss_list) == 2:
            s_ap, s_b = st(name + "_s")
            op(POOL, lambda: nc.gpsimd.tensor_tensor(out=s_ap[:P], in0=ss_list[0][0][:P],
                                                     in1=ss_list[1][0][:P], op=ALU.add),
               reads=[ss_list[0][1], ss_list[1][1]], writes=[s_b])
            src_ap, src_b = s_ap, s_b
        op(POOL, lambda: nc.gpsimd.tensor_scalar(out=t_ap[:P], in0=src_ap[:P], scalar1=1.0 / n,
                                                 scalar2=EPS, op0=ALU.mult, op1=ALU.add),
           reads=[src_b], writes=[t_b])
        op(POOL, lambda: nc.gpsimd.tensor_tensor(out=r_ap[:P], in0=t_ap[:P], in1=nhalf[:P], op=ALU.pow),
           reads=[t_b, bf("nhalf")], writes=[r_b])
        return r_ap, r_b

    def transposes_to(P, src_bf, src_buf, gT, dstT, dst_buf, tok0):
        yield from slack(S_TR)
        yield from acquire("pT", 0)
        for k in range(8):
            op(PE, lambda: nc.tensor.transpose(out=pT[:, k, 0:P], in_=src_bf[:, k * 128:(k + 1) * 128],
                                               identity=ident[:P, :P]),
               reads=[src_buf, bf("ident")], writes=[B_pT], signal=(k == 7))
        yield from slack(3)
        op(DVE, lambda: nc.vector.tensor_tensor(
            out=dstT[:, :, tok0:tok0 + P], in0=pT[:, :, 0:P],
            in1=gT[:, 0:8].unsqueeze(2).broadcast_to([128, 8, P]), op=ALU.mult),
           reads=[B_pT, bf("GT")], writes=[dst_buf])
        release("pT")
        yield

    def norm_to(P, x_ap, x_buf, gT, dstT, dst_buf, tok0, name, c=0):
        name = name + "_%d" % c
        hb = tA[c]
        hbb = bf("tA%d" % c)
        ss_ap, ss_b = st(name + "_ss")
        op(ACT, lambda: nc.scalar.activation(out=hb[:P], in_=x_ap, func=AF.Square, accum_out=ss_ap[:P]),
           reads=list(x_buf), writes=[hbb, ss_b])
        r_ap, r_b = rstd_from(P, [(ss_ap, ss_b)], D, name)
        op(DVE, lambda: nc.vector.tensor_scalar(out=hb[:P], in0=x_ap, scalar1=r_ap[:P], scalar2=None,
                                                op0=ALU.mult),
           reads=list(x_buf) + [r_b], writes=[hbb])
        yield from transposes_to(P, hb[:P, :], hbb, gT, dstT, dst_buf, tok0)

    def kv_token_major(P, blk, tok0, Wt, Wb, slot, need_f32=False):
        ps = bank[M2]
        for k in range(8):
            op(PE, lambda: nc.tensor.matmul(ps[:P, 0:256], lhsT=actT[:, k, tok0:tok0 + P],
                                            rhs=Wt[:, k, 0:256], start=(k == 0), stop=(k == 7)),
               reads=[B_actT[blk], Wb], writes=[B_bank[M2]], signal=(k == 7))
        yield
        if need_f32:
            op(ACT, lambda: nc.scalar.activation(out=kvf[:P], in_=ps[:P, 0:256], func=AF.Copy),
               reads=[B_bank[M2]], writes=[bf("kvf")])
        op(DVE, lambda: nc.vector.tensor_copy(
            out=Vaug[:P, slot, :, 0:64], in_=ps[:P, 128:256].rearrange("p (h d) -> p h d", d=64)),
           reads=[B_bank[M2]], writes=[B_V[slot]])
        yield

    def k_feature_major(T, nblk, Wt, Wb, col0, slots):
        ps = bank[M2]
        for k in range(8):
            op(PE, lambda: nc.tensor.matmul(ps[:, 0:T], lhsT=Wt[:, k, 0:128], rhs=actT[:, k, 0:T],
                                            start=(k == 0), stop=(k == 7)),
               reads=B_actT[:nblk] + [Wb], writes=[B_bank[M2]], signal=(k == 7))
            if k == 3:
                yield
        yield
        op(ACT, lambda: nc.scalar.activation(out=kT[:, col0:col0 + T], in_=ps[:, 0:T], func=AF.Copy),
           reads=[B_bank[M2]], writes=[B_kT[s] for s in slots])
        yield

    def halo_gen():
        Wt, Wb = ringM.open(P_KV)
        yield from norm_to(128, t_o, [bf("tB0_1"), bf("tB0_2")], gT_pre, actT, B_actT[0], 0, "n1")
        yield from k_feature_major(128, 1, Wt, Wb, 0, [0])
        yield from kv_token_major(128, 0, 0, Wt, Wb, 0)

    _drive(halo_gen())

    casts_mixer_rest()
    ringM.retire()
    for h in range(8):
        hk, r = h // 4, h % 4
        slope = 2.0 ** (-(h + 1))
        op(POOL, lambda: nc.gpsimd.tensor_scalar(out=tmpf, in0=dqk, scalar1=-8.0 * slope,
                                                 scalar2=0.0, op0=ALU.mult, op1=ALU.add),
           reads=[bf("dqk")], writes=[bf("tmpf")])
        op(POOL, lambda: nc.gpsimd.affine_select(out=biasC[:, hk, r * 128:(r + 1) * 128], in_=tmpf,
                                                 pattern=[[1, 128]], compare_op=ALU.is_ge, fill=NEG,
                                                 base=0, channel_multiplier=-1),
           reads=[bf("tmpf")], writes=[bf("biasC")])
        op(POOL, lambda: nc.gpsimd.tensor_scalar(out=tmpf2, in0=dqk, scalar1=-8.0 * slope,
                                                 scalar2=-1024.0 * slope, op0=ALU.mult, op1=ALU.add),
           reads=[bf("dqk")], writes=[bf("tmpf2")])
        op(POOL, lambda: nc.gpsimd.affine_select(out=biasP[:, hk, r * 128:(r + 1) * 128], in_=tmpf2,
                                                 pattern=[[-1, 128]], compare_op=ALU.is_ge, fill=NEG,
                                                 base=0, channel_multiplier=1),
           reads=[bf("tmpf2")], writes=[bf("biasP")])
        op(POOL, lambda: nc.gpsimd.tensor_scalar(out=tmpf[:, 0:4], in0=d4[:], scalar1=-8.0 * slope,
                                                 scalar2=0.0, op0=ALU.mult, op1=ALU.add),
           reads=[bf("d4")], writes=[bf("tmpf")])
        op(POOL, lambda: nc.gpsimd.affine_select(out=Tcache[:, h, 60:64], in_=tmpf[:, 0:4],
                                                 pattern=[[-1, 4]], compare_op=ALU.is_ge, fill=NEG,
                                                 base=0, channel_multiplier=1),
           reads=[bf("tmpf")], writes=[bf("Tcache")])
        op(POOL, lambda: nc.gpsimd.tensor_scalar(out=tmpf2[:64, 0:64], in0=dqk[:64, 0:64],
                                                 scalar1=-8.0 * slope, scalar2=0.0,
                                                 op0=ALU.mult, op1=ALU.add),
           reads=[bf("dqk")], writes=[bf("tmpf2")])
        op(POOL, lambda: nc.gpsimd.affine_select(out=tmpf[:64, 0:64], in_=tmpf2[:64, 0:64],
                                                 pattern=[[1, 64]], compare_op=ALU.is_ge, fill=NEG,
                                                 base=0, channel_multiplier=-1),
           reads=[bf("tmpf2")], writes=[bf("tmpf")])
        op(POOL, lambda: nc.gpsimd.affine_select(
            out=biasN[:64, h, :].rearrange("p (j t) -> p j t", t=4),
            in_=tmpf[:64, 0:64].rearrange("p (j t) -> p j t", t=4),
            pattern=[[-4, 16], [0, 4]], compare_op=ALU.is_ge, fill=NEG, base=0, channel_multiplier=1),
           reads=[bf("tmpf")], writes=[bf("biasN")])
    op(DVE, lambda: nc.vector.tensor_scalar(out=biasP0[:], in0=biasP[:], scalar1=flag[:, 0:1],
                                            scalar2=None, op0=ALU.add),
       reads=[bf("biasP")], writes=[bf("biasP0")])
    op(DVE, lambda: nc.vector.tensor_copy(out=Rsel[:].rearrange("p (j t) -> p j t", t=4),
                                          in_=ident[0:4, 0:4].unsqueeze(1).broadcast_to([4, 16, 4])),
       reads=[bf("ident")], writes=[bf("Rsel")])
    op(DVE, lambda: nc.vector.tensor_copy(out=wsm[:].rearrange("p (g t) -> p g t", t=4), in_=wTm[0:4, :, 0:4]),
       reads=[bf("wTm")], writes=[bf("wsm")])
    op(PE, lambda: nc.tensor.matmul(bank[M2][:64, 0:32], lhsT=Rsel[:, :], rhs=wsm[:, :], start=True, stop=True),
       reads=[bf("Rsel"), bf("wsm")], writes=[B_bank[M2]])
    wTs_f4 = wTs_f.rearrange("p g (j t) -> p g j t", t=4)
    op(DVE, lambda: nc.vector.tensor_copy(
        out=wTs_f4,
        in_=bank[M2][:64, 0:32].rearrange("p (g t) -> p g t", t=4).unsqueeze(2).broadcast_to([64, 8, 16, 4])),
       reads=[B_bank[M2]], writes=[bf("wTs_f")])
    op(POOL, lambda: nc.gpsimd.affine_select(out=wTs_f4, in_=wTs_f4,
                                             pattern=[[0, 8], [-4, 16], [0, 4]], compare_op=ALU.is_ge,
                                             fill=0.0, base=0, channel_multiplier=1),
       reads=[bf("wTs_f")], writes=[bf("wTs_f")])
    op(POOL, lambda: nc.gpsimd.affine_select(out=wTs[:].rearrange("p g (j t) -> p g j t", t=4), in_=wTs_f4,
                                             pattern=[[0, 8], [4, 16], [0, 4]], compare_op=ALU.is_ge,
                                             fill=0.0, base=3, channel_multiplier=-1),
       reads=[bf("wTs_f")], writes=[bf("wTs"), bf("relu0"), bf("relu1")])


    def rr(*gens):
        active = list(gens)
        while active:
            for g in list(active):
                try:
                    next(g)
                except StopIteration:
                    active.remove(g)
            yield

    def mixer_gen(sbi, P, nblk, sample):
        T = P * nblk
        pos = 0 if sample else sbi + 1
        xs_i = pos % 2
        xb = xbuf[xs_i]
        Bx = B_x[xs_i]
        last_prompt = (not sample) and sbi == NSB - 1

        def chainA(c):
            for b in range(c, nblk, 2):
                yield from norm_to(P, xb[:P, b, :], [Bx[b]], gT_pre, actT, B_actT[b], b * P, "n1", c)

        yield from rr(chainA(0), chainA(1))

        Wu, Wub = ringM.open(P_U)
        Wv, Wvb = ringM.open(P_V)

        def chainB(c):
            u_t = tB[c][:, 0:512]
            v_t = tB[c][:, 512:1024]
            a_t = tB[c][:, 1024:1536]
            Bu, Bv, Ba = bf("tB%d_0" % c), bf("tB%d_1" % c), bf("tB%d_2" % c)
            v_bf = v_bfs[c]
            Bvb = bf("v_bf%d" % c)
            bnst, mv = bnsts[c], mvs[c]
            Bbn, Bmv = bf("bnst%d" % c), bf("mv%d" % c)
            for b in range(c, nblk, 2):
                tk = slice(b * P, (b + 1) * P)
                for n, (Wt, Wb) in enumerate(((Wu, Wub), (Wv, Wvb))):
                    yield from acquire(MA[n], c)
                    for k in range(8):
                        op(PE, lambda: nc.tensor.matmul(bank[MA[n]][:P, :], lhsT=actT[:, k, tk], rhs=Wt[:, k, :],
                                                        start=(k == 0), stop=(k == 7)),
                           reads=[B_actT[b], Wb], writes=[B_bank[MA[n]]], signal=(k == 7))
                    yield from slack(4)
                    dst_t, dst_b = (u_t, Bu) if n == 0 else (v_t, Bv)
                    op(ACT, lambda: nc.scalar.activation(out=dst_t[:P], in_=bank[MA[n]][:P, :], func=AF.Gelu_apprx_tanh),
                       reads=[B_bank[MA[n]]], writes=[dst_b])
                    release(MA[n])
                    yield
                op(DVE, lambda: nc.vector.bn_stats(out=bnst[:P], in_=v_t[:P]), reads=[Bv], writes=[Bbn])
                op(DVE, lambda: nc.vector.bn_aggr(out=mv[:P], in_=bnst[:P]), reads=[Bbn], writes=[Bmv])
                yield
                rv_ap, rv_b = rstd_from(P, [(mv[:, 1:2], Bmv)], 1.0, "lnv%d" % c)
                yield
                op(DVE, lambda: nc.vector.tensor_scalar(out=v_t[:P], in0=v_t[:P], scalar1=mv[:P, 0:1],
                                                        scalar2=rv_ap[:P], op0=ALU.subtract, op1=ALU.mult),
                   reads=[Bv, Bmv, rv_b], writes=[Bv])
                yield
                op(POOL, lambda: nc.gpsimd.tensor_tensor(out=v_t[:P], in0=v_t[:P], in1=lng_bc[:P], op=ALU.mult),
                   reads=[Bv], writes=[Bv])
                yield
                need_f32 = sample or (last_prompt and b == nblk - 1)
                if need_f32:
                    op(POOL, lambda: nc.gpsimd.tensor_tensor(out=v_t[:P], in0=v_t[:P], in1=lnb_bc[:P], op=ALU.add),
                       reads=[Bv], writes=[Bv])
                    yield
                    op(DVE, lambda: nc.vector.tensor_copy(out=v_bf[:P], in_=v_t[:P]), reads=[Bv], writes=[Bvb])
                    if sample:
                        dma(SP, new_out_sem(), cvs[:, :], v_t[:P], reads=[Bv])
                    else:
                        dma(SP, new_out_sem(), cvp[:, :], v_t[:P], reads=[Bv])
                else:
                    op(POOL, lambda: nc.gpsimd.tensor_tensor(out=v_bf[:P], in0=v_t[:P], in1=lnb_bc[:P], op=ALU.add),
                       reads=[Bv], writes=[Bvb])
                yield
                wsp = wTs if sample else wTm
                wspb = bf("wTs") if sample else bf("wTm")
                yield from slack(S_SP)
                yield from acquire(M2, c)
                for g in range(8):
                    op(PE, lambda: nc.tensor.matmul(bank[M2][:P, g * 64:(g + 1) * 64], lhsT=wsp[:P, g, 0:P],
                                                    rhs=v_bf[:P, g * 64:(g + 1) * 64], start=True, stop=True),
                       reads=[Bvb, wspb], writes=[B_bank[M2]], signal=(g == 7))
                yield from slack(3)
                bt = btok_s if sample else btok
                op(DVE, lambda: nc.vector.tensor_tensor(
                    out=a_t[:P].rearrange("p (g d) -> p g d", d=64),
                    in0=bank[M2][:P, :].rearrange("p (g d) -> p g d", d=64),
                    in1=bt[:P, 0:8].unsqueeze(2).broadcast_to([P, 8, 64]), op=ALU.add),
                   reads=[B_bank[M2], bf("GT"), bf("btok_s")], writes=[Ba])
                release(M2)
                yield
                op(POOL, lambda: nc.gpsimd.tensor_tensor(out=a_t[:P], in0=a_t[:P], in1=u_t[:P], op=ALU.mult),
                   reads=[Ba, Bu], writes=[Ba])
                yield
                ssa_ap, ssa_b = st("ssa%d" % c)
                op(ACT, lambda: nc.scalar.activation(out=mrg4[:P, b, 0:512], in_=a_t[:P], func=AF.Square,
                                                     accum_out=ssa_ap[:P]),
                   reads=[Ba], writes=[bf("mrg%d" % b), ssa_b])
                yield
                ra_ap, ra_b = rstd_from(P, [(ssa_ap, ssa_b)], 512, "rsa%d" % c)
                yield
                op(DVE, lambda: nc.vector.tensor_scalar(out=mrg4[:P, b, 0:512], in0=a_t[:P], scalar1=ra_ap[:P],
                                                        scalar2=None, op0=ALU.mult),
                   reads=[Ba, ra_b], writes=[bf("mrg%d" % b)])
                yield

        yield from rr(chainB(0), chainB(1))
        ringM.retire()
        ringM.retire()

        Wq, Wqb = ringM.open(P_Q)
        for c in range(4):
            pb = MA[c % 2]
            for k in range(8):
                op(PE, lambda: nc.tensor.matmul(bank[pb][:, 0:T], lhsT=Wq[:, k, c * 128:(c + 1) * 128],
                                                rhs=actT[:, k, 0:T], start=(k == 0), stop=(k == 7)),
                   reads=B_actT[:nblk] + [Wqb], writes=[B_bank[pb]], signal=(k == 7))
                if k == 3 or k == 7:
                    yield
            if c % 2 == 0:
                op(ACT, lambda: nc.scalar.activation(out=qT[:, c, 0:T], in_=bank[pb][:, 0:T], func=AF.Copy),
                   reads=[B_bank[pb]], writes=[B_qT[c]])
            else:
                op(DVE, lambda: nc.vector.tensor_copy(out=qT[:, c, 0:T], in_=bank[pb][:, 0:T]),
                   reads=[B_bank[pb]], writes=[B_qT[c]])
        ringM.retire()
        Wkv, Wkvb = ringM.open(P_KV)
        yield from k_feature_major(T, nblk, Wkv, Wkvb, 128, list(range(1, 1 + nblk)))
        for b in range(nblk):
            yield from kv_token_major(P, b, b * P, Wkv, Wkvb, 1 + b,
                                      need_f32=(sample or (last_prompt and b == nblk - 1)))
            if sample:
                dma(SP, new_out_sem(), wks[:, 124:128, :], kvf[:P, 0:128], reads=[bf("kvf")])
                dma(SP, new_out_sem(), wvs[:, 124:128, :], kvf[:P, 128:256], reads=[bf("kvf")])
            elif last_prompt and b == nblk - 1:
                dma(SP, new_out_sem(), wkp[:, :], kvf[:P, 0:128], reads=[bf("kvf")])
                dma(SP, new_out_sem(), wvp[:, :], kvf[:P, 128:256], reads=[bf("kvf")])
        ringM.retire()

        def norm_attn(c, b, b_t, Bb, zz, rz, Bzz, Brz, pbanks):
            for hk in range(2):
                pb = pbanks[hk]
                pov = bank[pb][:P, 0:260].rearrange("p (r e) -> p r e", e=65)
                op(DVE, lambda: nc.vector.tensor_tensor(out=zz[:P, hk * 4:(hk + 1) * 4], in0=pov[:, :, 64],
                                                        in1=esink[:P, hk * 4:(hk + 1) * 4], op=ALU.add),
                   reads=[B_bank[pb], bf("esink")], writes=[Bzz])
                op(DVE, lambda: nc.vector.reciprocal(out=rz[:P, hk * 4:(hk + 1) * 4], in_=zz[:P, hk * 4:(hk + 1) * 4]),
                   reads=[Bzz], writes=[Brz])
                op(DVE, lambda: nc.vector.tensor_tensor(
                    out=b_t[:P, hk * 256:(hk + 1) * 256].rearrange("p (r d) -> p r d", d=64),
                    in0=pov[:, :, 0:64],
                    in1=rz[:P, hk * 4:(hk + 1) * 4].unsqueeze(2).broadcast_to([P, 4, 64]), op=ALU.mult),
                   reads=[B_bank[pb], Brz], writes=[Bb])

        def chainD(c):
            b_t = tB[c][:, 0:512]
            Bb = bf("tB%d_0" % c)
            zz, rz = zzs[c], rzs[c]
            Bzz, Brz = bf("zz%d" % c), bf("rz%d" % c)
            for b in range(c, nblk, 2):
                if not sample:
                    PTt = PTb[b % 2]
                    BPT = B_PT[b % 2]
                    qc = slice(b * 128, (b + 1) * 128)
                    for hk in range(2):
                        hp = slice(hk * 64, (hk + 1) * 64)
                        for kb in range(2):
                            pb = MA[kb]
                            if kb == 0:
                                bias_t, bias_b = (biasP0, bf("biasP0")) if b == 0 and sbi == 0 else (biasP, bf("biasP"))
                            else:
                                bias_t, bias_b = biasC, bf("biasC")
                            kc = slice((b + kb) * 128, (b + kb + 1) * 128)
                            yield from acquire(pb, c)
                            op(PE, lambda: nc.tensor.matmul(bank[pb][:, :], lhsT=ident[:], rhs=bias_t[:, hk, :],
                                                            start=True, stop=False, skip_group_check=True),
                               reads=[bias_b, bf("ident")], writes=[B_bank[pb]], signal=False)
                            for r in range(4):
                                op(PE, lambda: nc.tensor.matmul(bank[pb][:, r * 128:(r + 1) * 128], lhsT=kT[hp, kc],
                                                                rhs=qT[hp, r, qc], start=False, stop=(r == 3),
                                                                skip_group_check=True),
                                   reads=[B_kT[b + kb], B_qT[r]], writes=[B_bank[pb]], signal=(r == 3))
                            yield from slack(2)
                            op(ACT, lambda: nc.scalar.activation(out=PTt[:, hk * 2 + kb, :], in_=bank[pb][:, :],
                                                                 func=AF.Exp, scale=0.125),
                               reads=[B_bank[pb]], writes=[BPT])
                            release(pb)
                            yield
                    for hk in range(2):
                        yield from acquire(M2, c)
                        for r in range(4):
                            for kb in range(2):
                                op(PE, lambda: nc.tensor.matmul(bank[M2][:, r * 65:(r + 1) * 65],
                                                                lhsT=PTt[:, hk * 2 + kb, r * 128:(r + 1) * 128],
                                                                rhs=Vaug[:, b + kb, hk, 0:65], start=(kb == 0), stop=(kb == 1)),
                                   reads=[BPT, B_V[b + kb]], writes=[B_bank[M2]], signal=(kb == 1 and r == 3))
                        yield from slack(2)
                        pov = bank[M2][:P, 0:260].rearrange("p (r e) -> p r e", e=65)
                        op(DVE, lambda: nc.vector.tensor_tensor(out=zz[:P, hk * 4:(hk + 1) * 4], in0=pov[:, :, 64],
                                                                in1=esink[:P, hk * 4:(hk + 1) * 4], op=ALU.add),
                           reads=[B_bank[M2], bf("esink")], writes=[Bzz])
                        op(DVE, lambda: nc.vector.reciprocal(out=rz[:P, hk * 4:(hk + 1) * 4], in_=zz[:P, hk * 4:(hk + 1) * 4]),
                           reads=[Bzz], writes=[Brz])
                        op(DVE, lambda: nc.vector.tensor_tensor(
                            out=b_t[:P, hk * 256:(hk + 1) * 256].rearrange("p (r d) -> p r d", d=64),
                            in0=pov[:, :, 0:64],
                            in1=rz[:P, hk * 4:(hk + 1) * 4].unsqueeze(2).broadcast_to([P, 4, 64]), op=ALU.mult),
                           reads=[B_bank[M2], Brz], writes=[Bb])
                        release(M2)
                        yield
                else:
                    POb = [M2, FB[0]]
                    for hk in range(2):
                        op(PE, lambda: nc.tensor.matmul(bank[POb[hk]][:P, 0:260], lhsT=zeros[:, 0:P], rhs=zeros[:, 0:260],
                                                        start=True, stop=False, skip_group_check=True),
                           reads=[bf("zeros")], writes=[B_bank[POb[hk]]], signal=False)
                    for j in range(SEQ_CORE + 1):
                        new = (j == SEQ_CORE)
                        KP = 64 if new else 128
                        PTt = PTb[j % 2]
                        BPT = B_PT[j % 2]
                        if new:
                            rhs_bias = biasN[:, :, :].rearrange("p h q -> p (h q)")
                            bias_b = bf("biasN")
                        else:
                            bj = biasJ[j % 2]
                            bias_b = bf("biasJ%d" % (j % 2))
                            op(POOL, lambda: nc.gpsimd.tensor_copy(out=bj[:].rearrange("p (h q) -> p h q", q=64),
                                                                   in_=Tcache[:, :, 60 - 4 * j:124 - 4 * j]),
                               reads=[bf("Tcache")], writes=[bias_b])
                            rhs_bias = bj[:, :]
                        for hk in range(2):
                            pb = MA[hk]
                            hp = slice(hk * 64, (hk + 1) * 64)
                            op(PE, lambda: nc.tensor.matmul(bank[pb][:KP, 0:256], lhsT=ident[:, :KP],
                                                            rhs=rhs_bias[:, hk * 256:(hk + 1) * 256],
                                                            start=True, stop=False, skip_group_check=True),
                               reads=[bias_b, bf("ident")], writes=[B_bank[pb]], signal=False)
                            for r in range(4):
                                lhs = kT[hp, 128:128 + 64] if new else kTc(hp, j)
                                op(PE, lambda: nc.tensor.matmul(bank[pb][:KP, r * 64:(r + 1) * 64], lhsT=lhs,
                                                                rhs=qT[hp, r, 0:64], start=False, stop=(r == 3),
                                                                skip_group_check=True),
                                   reads=[B_kT[1], bf("kTc"), B_qT[r]], writes=[B_bank[pb]], signal=(r == 3))
                            op(ACT, lambda: nc.scalar.activation(out=PTt[:KP, 0, hk * 256:(hk + 1) * 256],
                                                                 in_=bank[pb][:KP, 0:256], func=AF.Exp, scale=0.125),
                               reads=[B_bank[pb]], writes=[BPT])
                        for h in range(8):
                            hk, r = h // 4, h % 4
                            rhs = Vaug[:KP, 1, hk, 0:65] if new else Vc(j)[:, hk, 0:65]
                            op(PE, lambda: nc.tensor.matmul(bank[POb[hk]][:P, r * 65:(r + 1) * 65],
                                                            lhsT=PTt[:KP, 0, h * 64:(h + 1) * 64], rhs=rhs,
                                                            start=False, stop=new, skip_group_check=True),
                               reads=[BPT, B_V[1], bf("Vc")], writes=[B_bank[POb[hk]]], signal=(h == 3 or h == 7))
                        yield
                    norm_attn(c, b, b_t, Bb, zz, rz, Bzz, Brz, POb)
                ssb_ap, ssb_b = st("ssb%d" % c)
                op(ACT, lambda: nc.scalar.activation(out=mrg4[:P, b, 512:1024], in_=b_t[:P], func=AF.Square,
                                                     accum_out=ssb_ap[:P]),
                   reads=[Bb], writes=[bf("mrg%d" % b), ssb_b])
                yield
                rb_ap, rb_b = rstd_from(P, [(ssb_ap, ssb_b)], 512, "rsb%d" % c)
                yield
                op(DVE, lambda: nc.vector.tensor_scalar(out=mrg4[:P, b, 512:1024], in0=b_t[:P], scalar1=rb_ap[:P],
                                                        scalar2=None, op0=ALU.mult),
                   reads=[Bb, rb_b], writes=[bf("mrg%d" % b)])
                yield
                yield from transposes_to(P, mrg4[:P, b, :], bf("mrg%d" % b), gT_mrg, actT, B_actT[b], b * P)

        yield from rr(chainD(0), chainD(1))
        if not sample:
            op(POOL, lambda: nc.gpsimd.tensor_copy(out=kT[:, 0:128], in_=kT[:, 512:640]),
               reads=[B_kT[4]], writes=[B_kT[0]])
            op(POOL, lambda: nc.gpsimd.tensor_copy(out=Vaug[:, 0, :, :], in_=Vaug[:, 4, :, :]),
               reads=[B_V[4]], writes=[B_V[0]])

        Wo0, Wo0b = ringM.open(P_O0)
        Wo1, Wo1b = ringM.open(P_O1)

        def chainE(c):
            t_oc = tB[c][:, 512:1536]
            Bto = [bf("tB%d_1" % c), bf("tB%d_2" % c)]
            hb = tA[c]
            hbb = bf("tA%d" % c)
            for b in range(c, nblk, 2):
                tk = slice(b * P, (b + 1) * P)
                ss = []
                for n, (Wt, Wb) in enumerate(((Wo0, Wo0b), (Wo1, Wo1b))):
                    yield from acquire(MA[n], c)
                    for k in range(8):
                        op(PE, lambda: nc.tensor.matmul(bank[MA[n]][:P, :], lhsT=actT[:, k, tk], rhs=Wt[:, k, :],
                                                        start=(k == 0), stop=(k == 7)),
                           reads=[B_actT[b], Wb], writes=[B_bank[MA[n]]], signal=(k == 7))
                    yield from slack(4)
                    s_ap, s_b = st("sso%d_%d" % (n, c))
                    op(ACT, lambda: nc.scalar.activation(out=hb[:P, n * 512:(n + 1) * 512], in_=bank[MA[n]][:P, :],
                                                         func=AF.Square, accum_out=s_ap[:P]),
                       reads=[B_bank[MA[n]]], writes=[hbb, s_b])
                    ss.append((s_ap, s_b))
                    op(DVE, lambda: nc.vector.tensor_tensor(out=t_oc[:P, n * 512:(n + 1) * 512], in0=bank[MA[n]][:P, :],
                                                            in1=gpost_bc[:P, n * 512:(n + 1) * 512], op=ALU.mult),
                       reads=[B_bank[MA[n]]], writes=[Bto[n]])
                    release(MA[n])
                    yield
                ro_ap, ro_b = rstd_from(P, ss, D, "rso%d" % c)
                yield
                op(DVE, lambda: nc.vector.scalar_tensor_tensor(out=xb[:P, b, :], in0=t_oc[:P], scalar=ro_ap[:P],
                                                               in1=xb[:P, b, :], op0=ALU.mult, op1=ALU.add),
                   reads=Bto + [ro_b, Bx[b]], writes=[Bx[b]])
                yield

        yield from rr(chainE(0), chainE(1))
        ringM.retire()
        ringM.retire()

        def chainF(c):
            for b in range(c, nblk, 2):
                yield from norm_to(P, xb[:P, b, :], [Bx[b]], gT_ffn, x2T, B_x2T[b], b * P, "n2", c)

        yield from rr(chainF(0), chainF(1))

    def ffn_gen(sbi, P, nblk, sample):
        T = P * nblk
        pos = 0 if sample else sbi + 1
        xs_i = pos % 2
        xb = xbuf[xs_i]
        Bx = B_x[xs_i]
        row0 = 0 if sample else sbi * 512
        def up_evac(f):
            pb = FB[f % 4]
            rt = relu_t[f % 2]
            rb_ = bf("relu%d" % (f % 2))
            op(ACT, lambda: nc.scalar.activation(out=rt[:, 0:T], in_=bank[pb][:, 0:T], func=AF.Relu),
               reads=[B_bank[pb]], writes=[rb_])
            op(DVE, lambda: nc.vector.tensor_tensor(out=h2T[:, f, 0:T], in0=rt[:, 0:T], in1=rt[:, 0:T],
                                                    op=ALU.mult),
               reads=[rb_], writes=[B_h2T[f]])

        pending = None
        for j in range(8):
            for half in range(2):
                Wt, Wb = ringF.open(P_UP[j])
                for fc in range(2):
                    f = j * 4 + half * 2 + fc
                    pb = FB[f % 4]
                    for k in range(8):
                        op(PE, lambda: nc.tensor.matmul(bank[pb][:, 0:T], lhsT=Wt[:, k, fc * 128:(fc + 1) * 128],
                                                        rhs=x2T[:, k, 0:T], start=(k == 0), stop=(k == 7)),
                           reads=B_x2T[:nblk] + [Wb], writes=[B_bank[pb]], signal=(k == 7))
                        if k in (1, 3, 5):
                            yield
                    if pending is not None:
                        up_evac(pending)
                    pending = f
                    yield
                ringF.retire()
        up_evac(pending)
        ssf = [[None, None] for _ in range(nblk)]
        for c in range(2):
            for kg in range(4):
                for half in range(2):
                    Wt, Wb = ringF.open(P_DN[c][kg])
                    for b in range(nblk):
                        tk = slice(b * P, (b + 1) * P)
                        pb = FB[b]
                        for kk in range(4):
                            f = kg * 8 + half * 4 + kk
                            first = (kg == 0 and half == 0 and kk == 0)
                            last = (kg == 3 and half == 1 and kk == 3)
                            op(PE, lambda: nc.tensor.matmul(bank[pb][:P, :], lhsT=h2T[:, f, tk], rhs=Wt[:, kk, :],
                                                            start=first, stop=last),
                               reads=[B_h2T[f], Wb], writes=[B_bank[pb]], signal=(kk == 3))
                            if kk == 1:
                                yield
                        yield
                    ringF.retire()
            for b in range(nblk):
                pb = FB[b]
                s_ap, s_b = st("ssf%d_%d" % (b, c))
                op(ACT, lambda: nc.scalar.activation(out=fsink[:P, :], in_=bank[pb][:P, :], func=AF.Square,
                                                     accum_out=s_ap[:P]),
                   reads=[B_bank[pb]], writes=[bf("fsink"), s_b])
                ssf[b][c] = (s_ap, s_b)
                if c == 0:
                    op(DVE, lambda: nc.vector.tensor_tensor(out=f0g[:P, b, :], in0=bank[pb][:P, :],
                                                            in1=gpostffn_bc[:P, 0:512], op=ALU.mult),
                       reads=[B_bank[pb]], writes=[bf("f0g%d" % b)])
                else:
                    rf_ap, rf_b = rstd_from(P, ssf[b], D, "rsf")
                    op(DVE, lambda: nc.vector.tensor_tensor(out=tmp_y[:P], in0=bank[pb][:P, :],
                                                            in1=gpostffn_bc[:P, 512:1024], op=ALU.mult),
                       reads=[B_bank[pb]], writes=[bf("tmp_y")])
                    op(DVE, lambda: nc.vector.scalar_tensor_tensor(out=xb[:P, b, 512:1024], in0=tmp_y[:P],
                                                                   scalar=rf_ap[:P], in1=xb[:P, b, 512:1024],
                                                                   op0=ALU.mult, op1=ALU.add),
                       reads=[bf("tmp_y"), rf_b, Bx[b]], writes=[Bx[b]])
                    op(DVE, lambda: nc.vector.scalar_tensor_tensor(out=xb[:P, b, 0:512], in0=f0g[:P, b, :],
                                                                   scalar=rf_ap[:P], in1=xb[:P, b, 0:512],
                                                                   op0=ALU.mult, op1=ALU.add),
                       reads=[bf("f0g%d" % b), rf_b, Bx[b]], writes=[Bx[b]])
                    dst = ys if sample else yp
                    dma(SP, S_xstore[xs_i][b], dst[row0 + b * P:row0 + (b + 1) * P, :], xb[:P, b, :],
                        reads=[Bx[b]])
                    if pos + 1 < NSB and not sample:
                        nx = pos + 1
                        lbs = [b - 1] if b >= 1 else []
                        if b == nblk - 1:
                            lbs.append(b)
                        for lb in lbs:
                            dma(SP, S_xload[xs_i][lb], xb[:, lb, :],
                                xp[nx * 512 + lb * 128:nx * 512 + (lb + 1) * 128, :], writes=[Bx[lb]])
                yield
        nxt = pos + 1
        if nxt < NSB and sample:
            for b in range(4):
                dma(SP, S_xload[xs_i][b], xb[:, b, :], xp[nxt * 512 + b * 128:nxt * 512 + (b + 1) * 128, :],
                    writes=[Bx[b]])

    dma(SP, new_out_sem(), wks[:, 0:124, :], ck[:, 4:128, :])
    dma(SP, new_out_sem(), wvs[:, 0:124, :], cv[:, 4:128, :])
    op(POOL, lambda: nc.gpsimd.memset(h2T[:, 16:32, 64:196], 1.0), writes=[bf("Vc")] + B_h2T[16:32])
    for half in range(2):
        ckb = PTb[1][:, half * 2:(half + 1) * 2, :]
        op(DVE, lambda: nc.vector.tensor_copy(out=ckb, in_=xb1[:, half, :].rearrange("p (a c) -> p a c", c=512)),
           reads=[B_x[1][half]], writes=[B_PT[1]])
        for jj in range(8):
            srcap = PTb[1][:, half * 2 + jj // 4, (jj % 4) * 128:(jj % 4 + 1) * 128]
            op(PE, lambda: nc.tensor.transpose(out=pT[:, jj, :], in_=srcap, identity=ident[:]),
               reads=[B_PT[1], bf("ident")], writes=[B_pT], signal=(jj == 7))
        op(DVE, lambda: nc.vector.tensor_copy(out=h2T[:, half * 8:(half + 1) * 8, 128:256], in_=pT[:]),
           reads=[B_pT], writes=[bf("kTc")] + B_h2T[half * 8:(half + 1) * 8])
        op(DVE, lambda: nc.vector.tensor_copy(
            out=h2T[:, 16 + half * 8:16 + (half + 1) * 8, 64:196].rearrange("p j (h e) -> p j h e", e=66)[:, :, :, 0:64],
            in_=xb1[:, 2 + half, :].rearrange("p (j h d) -> p j h d", h=2, d=64)),
           reads=[B_x[1][2 + half], bf("Vc")], writes=[bf("Vc")])
    for b in range(4):
        dma(SP, S_xload[1][b], xb1[:, b, :], xp[b * 128:(b + 1) * 128, :], writes=[B_x[1][b]])

    def prologue_side():
        n = 0
        for pid, src_ap in deferred_casts:
            cast(pid, src_ap)
            n += 1
            if n == 4:
                for _ in range(RING_F):
                    ringF.load_next()
            yield

    _drive(mixer_gen(NSB, 64, 1, True), prologue_side(), ratio=0.25)
    _drive(ffn_gen(NSB, 64, 1, True), mixer_gen(0, 128, 4, False), ratio=1.1)
    for sbi in range(NSB):
        nxt = mixer_gen(sbi + 1, 128, 4, False) if sbi + 1 < NSB else None
        _drive(ffn_gen(sbi, 128, 4, False), nxt, ratio=1.0, delay=6)

    for s in out_sems:
        if s.total > 0:
            SP.e.wait_ge(s.h, s.total)


_CACHE = {}


def kernel(**inputs):
    x_prompt = np.asarray(inputs["x_prompt"], dtype=np.float32)
    x_sample = np.asarray(inputs["x_sample"], dtype=np.float32)
    cache_k = np.asarray(inputs["cache_win_k"], dtype=np.float32)
    cache_v = np.asarray(inputs["cache_win_v"], dtype=np.float32)

    def w(name, shape):
        return np.ascontiguousarray(np.asarray(inputs[name], dtype=np.float32).reshape(shape))

    shared = {
        "w_in": w("w_in", (D, 1792)),
        "g_pre_mix": w("g_pre_mix", (1, D)),
        "ln_v_g": w("ln_v_g", (1, 512)),
        "ln_v_b": w("ln_v_b", (1, 512)),
        "w_spatial": w("w_spatial", (8, 128, 128)),
        "b_spatial": w("b_spatial", (8, 128)),
        "attn_sinks": w("attn_sinks", (1, 8)),
        "g_out_chunk": w("g_out_chunk", (1, 512)),
        "g_out_attn": w("g_out_attn", (1, 512)),
        "w_o": w("w_o", (D, D)),
        "g_post_mix": w("g_post_mix", (1, D)),
        "g_pre_ffn": w("g_pre_ffn", (1, D)),
        "w_up": w("w_up", (D, 4096)),
        "w_down": w("w_down", (4096, D)),
        "g_post_ffn": w("g_post_ffn", (1, D)),
    }
    in_maps = []
    for c in range(NCORES):
        b, half = c // 2, c % 2
        m = dict(shared)
        m["xp"] = np.ascontiguousarray(x_prompt[b, half * TOK_CORE:(half + 1) * TOK_CORE])
        if half == 1:
            m["xh"] = np.ascontiguousarray(x_prompt[b, TOK_CORE - 128:TOK_CORE])
            m["flag"] = np.zeros((128, 1), np.float32)
        else:
            m["xh"] = np.zeros((128, D), np.float32)
            m["flag"] = np.full((128, 1), NEG, np.float32)
        m["xs"] = np.ascontiguousarray(x_sample[c * SEQ_CORE:(c + 1) * SEQ_CORE].reshape(64, D))
        m["ck"] = np.ascontiguousarray(cache_k[0, c * SEQ_CORE:(c + 1) * SEQ_CORE].reshape(SEQ_CORE, 128, 128))
        m["cv"] = np.ascontiguousarray(cache_v[0, c * SEQ_CORE:(c + 1) * SEQ_CORE].reshape(SEQ_CORE, 128, 128))
        in_maps.append(m)

    if "nc" not in _CACHE:
        _CACHE["nc"] = build_program()
    nc, _es = _CACHE["nc"]
    res = run_bass_kernel_spmd(nc, in_maps, core_ids=list(range(NCORES)))
    R = res.results

    y_prompt = np.empty((4, 8192, D), np.float32)
    y_sample = np.empty((128, 4, D), np.float32)
    wk_p = np.empty((1, 4, 128, 2, 64), np.float32)
    wv_p = np.empty((1, 4, 128, 2, 64), np.float32)
    cv_p = np.empty((1, 4, 128, 512), np.float32)
    wk_s = np.empty((1, 128, 128, 2, 64), np.float32)
    wv_s = np.empty((1, 128, 128, 2, 64), np.float32)
    cv_s = np.empty((1, 128, 4, 512), np.float32)
    for c in range(NCORES):
        b, half = c // 2, c % 2
        r = R[c]
        y_prompt[b, half * TOK_CORE:(half + 1) * TOK_CORE] = r["yp"]
        y_sample[c * SEQ_CORE:(c + 1) * SEQ_CORE] = np.asarray(r["ys"]).reshape(SEQ_CORE, 4, D)
        if half == 1:
            wk_p[0, b] = np.asarray(r["wkp"]).reshape(128, 2, 64)
            wv_p[0, b] = np.asarray(r["wvp"]).reshape(128, 2, 64)
            cv_p[0, b] = r["cvp"]
        wk_s[0, c * SEQ_CORE:(c + 1) * SEQ_CORE] = np.asarray(r["wks"]).reshape(SEQ_CORE, 128, 2, 64)
        wv_s[0, c * SEQ_CORE:(c + 1) * SEQ_CORE] = np.asarray(r["wvs"]).reshape(SEQ_CORE, 128, 2, 64)
        cv_s[0, c * SEQ_CORE:(c + 1) * SEQ_CORE] = np.asarray(r["cvs"]).reshape(SEQ_CORE, 4, 512)
    return (y_prompt, y_sample, wk_p, wv_p, cv_p, wk_s, wv_s, cv_s)
```

```python
from contextlib import ExitStack

import numpy as np
import concourse.bass as bass
import concourse.mybir as mybir
from concourse.bass_utils import run_bass_kernel_spmd

F32 = mybir.dt.float32
BF16 = mybir.dt.bfloat16
AF = mybir.ActivationFunctionType
ALU = mybir.AluOpType

NCORES = 8
D = 1024
TOK_CORE = 4096
NSB = 8
SEQ_CORE = 16
EPS = 1e-6
NEG = -80000.0
RING_M = 3
RING_F = 4


class Sem:
    def __init__(self, nc, es, name):
        self.h = es.enter_context(nc.semaphore(name))
        self.total = 0


class Eng:
    def __init__(self, nc, es, eng, name, is_pe=False):
        self.e = eng
        self.sem = Sem(nc, es, "c_" + name)
        self.waited = {}
        self.is_pe = is_pe


class Buf:
    __slots__ = ("w", "r", "excl")

    def __init__(self, excl=False):
        self.w = None
        self.r = {}
        self.excl = excl


def _deps(E, reads, writes):
    deps = {}

    def add(s, v):
        if deps.get(s, 0) < v:
            deps[s] = v

    for b in reads:
        if b.w is not None:
            add(*b.w)
    for b in writes:
        if b.w is not None:
            add(*b.w)
        for s, v in b.r.items():
            add(s, v)
    for s, v in deps.items():
        if s is E.sem and E.is_pe:
            continue
        if E.waited.get(s, 0) >= v:
            continue
        E.e.wait_ge(s.h, v)
        E.waited[s] = v


def _reg(ev, reads, writes):
    for b in reads:
        if b.r.get(ev[0], 0) < ev[1]:
            b.r[ev[0]] = ev[1]
    for b in writes:
        b.w = ev
        b.r = {}


def op(E, fn, reads=(), writes=(), signal=True):
    if any(b.excl for b in reads):
        writes = list(writes) + [b for b in reads if b.excl]
        reads = [b for b in reads if not b.excl]
    _deps(E, reads, writes)
    inst = fn()
    if signal:
        E.sem.total += 1
        inst.then_inc(E.sem.h, 1)
        ev = (E.sem, E.sem.total)
    else:
        ev = (E.sem, E.sem.total + 1)
    _reg(ev, reads, writes)


def dma(Q, sem, out, in_, reads=(), writes=(), **kw):
    _deps(Q, reads, writes)
    inst = Q.e.dma_start(out=out, in_=in_, **kw)
    sem.total += 16
    inst.then_inc(sem.h, 16)
    _reg((sem, sem.total), reads, writes)


def build_program():
    nc = bass.Bass("TRN2", target_bir_lowering=False)
    es = ExitStack()
    _build(nc, es)
    return nc, es


STEP_COUNTS = []


def _drive(main, side=None, ratio=1.0, delay=0):
    nm = ns = 0
    acc = 0.0
    main_done = main is None
    side_done = side is None
    while not (main_done and side_done):
        if not main_done:
            try:
                next(main)
                nm += 1
            except StopIteration:
                main_done = True
        if not side_done and (nm > delay or main_done):
            acc += ratio if not main_done else 1e9
            while acc >= 1.0 and not side_done:
                acc -= 1.0
                try:
                    next(side)
                    ns += 1
                except StopIteration:
                    side_done = True
    STEP_COUNTS.append((nm, ns))


def _build(nc, es):
    def dram(name, shape, dt=F32, kind="ExternalInput"):
        return nc.dram_tensor(name, shape, dt, kind=kind).ap()

    def sb(name, shape, dt=F32):
        return es.enter_context(nc.sbuf_tensor(name, shape, dt))

    def psum(name, shape, dt=F32):
        return es.enter_context(nc.psum_tensor(name, shape, dt))

    xp = dram("xp", [TOK_CORE, D])
    xh = dram("xh", [128, D])
    flag_d = dram("flag", [128, 1])
    xs = dram("xs", [64, D])
    ck = dram("ck", [SEQ_CORE, 128, 128])
    cv = dram("cv", [SEQ_CORE, 128, 128])
    w_in = dram("w_in", [D, 1792])
    g_pre_mix = dram("g_pre_mix", [1, D])
    ln_v_g = dram("ln_v_g", [1, 512])
    ln_v_b = dram("ln_v_b", [1, 512])
    w_spatial = dram("w_spatial", [8, 128, 128])
    b_spatial = dram("b_spatial", [8, 128])
    attn_sinks = dram("attn_sinks", [1, 8])
    g_out_chunk = dram("g_out_chunk", [1, 512])
    g_out_attn = dram("g_out_attn", [1, 512])
    w_o = dram("w_o", [D, D])
    g_post_mix = dram("g_post_mix", [1, D])
    g_pre_ffn = dram("g_pre_ffn", [1, D])
    w_up = dram("w_up", [D, 4096])
    w_down = dram("w_down", [4096, D])
    g_post_ffn = dram("g_post_ffn", [1, D])

    yp = dram("yp", [TOK_CORE, D], kind="ExternalOutput")
    ys = dram("ys", [64, D], kind="ExternalOutput")
    wkp = dram("wkp", [128, 128], kind="ExternalOutput")
    wvp = dram("wvp", [128, 128], kind="ExternalOutput")
    cvp = dram("cvp", [128, 512], kind="ExternalOutput")
    wks = dram("wks", [SEQ_CORE, 128, 128], kind="ExternalOutput")
    wvs = dram("wvs", [SEQ_CORE, 128, 128], kind="ExternalOutput")
    cvs = dram("cvs", [64, 512], kind="ExternalOutput")

    NPIECE = 22
    scratch = dram("wscratch", [NPIECE, 128, 8, 512], BF16, kind="Internal")

    PE = Eng(nc, es, nc.tensor, "pe", is_pe=True)
    ACT = Eng(nc, es, nc.scalar, "act")
    DVE = Eng(nc, es, nc.vector, "dve")
    POOL = Eng(nc, es, nc.gpsimd, "pool")
    SP = Eng(nc, es, nc.sync, "sp")
    out_sems = []

    ident = sb("ident", [128, 128], BF16)
    zeros = sb("zeros", [128, 260], BF16)
    gpost_bc = sb("gpost_bc", [128, D])
    gpostffn_bc = sb("gpostffn_bc", [128, D])
    lng_bc = sb("lng_bc", [128, 512])
    lnb_bc = sb("lnb_bc", [128, 512])
    GT = sb("GT", [128, 32])
    gT_pre = GT[:, 0:8]
    gT_ffn = GT[:, 8:16]
    gT_mrg = GT[:, 16:24]
    esink = sb("esink", [128, 8])
    btok = GT[:, 24:32]
    btok_s = sb("btok_s", [64, 8])
    nhalf = sb("nhalf", [128, 1])
    flag = sb("flag_sb", [128, 1])
    wTm = sb("wTm", [128, 8, 128], BF16)
    wTs = sb("wTs", [64, 8, 64], BF16)
    Rsel = sb("Rsel", [4, 64], BF16)
    wsm = sb("wsm", [4, 32], BF16)
    biasJ = [sb("biasJ0", [128, 512], BF16), sb("biasJ1", [128, 512], BF16)]
    d4 = sb("d4", [128, 4])
    biasP = sb("biasP", [128, 2, 512], BF16)
    biasC = sb("biasC", [128, 2, 512], BF16)
    Tcache = sb("Tcache", [128, 8, 124], BF16)
    biasN = sb("biasN", [128, 8, 64], BF16)

    xbuf = [sb("xbuf0", [128, 4, D]), sb("xbuf1", [128, 4, D])]
    actT = sb("actT", [128, 8, 512], BF16)
    x2T = sb("x2T", [128, 8, 512], BF16)
    qT = sb("qT", [128, 4, 512], BF16)
    kT = sb("kT", [128, 640], BF16)
    Vaug = sb("Vaug", [128, 5, 2, 66], BF16)
    PTb = [sb("PTb0", [128, 4, 512], BF16), sb("PTb1", [128, 4, 512], BF16)]
    h2T = sb("h2T", [128, 32, 512], BF16)
    ringM_t = [sb("ringM%d" % i, [128, 8, 512], BF16) for i in range(RING_M)]
    ringF_t = [sb("ringF%d" % i, [128, 2048], BF16) for i in range(RING_F)]

    tA = [sb("tA0", [128, D], BF16), sb("tA1", [128, D], BF16)]
    tB = [sb("tB0", [128, 1536]), sb("tB1", [128, 1536])]
    v_bfs = [sb("v_bf0", [128, 512], BF16), sb("v_bf1", [128, 512], BF16)]
    fsink = sb("fsink", [128, 512], BF16)
    kvf = sb("kvf", [128, 256])
    mrg4 = sb("mrg4", [128, 4, D], BF16)
    h_bf = tA[0]
    junk = tA[1]
    t_o = tB[0][:, 512:1536]
    relu_t = [sb("relu0", [128, 512]), sb("relu1", [128, 512])]
    f0g = sb("f0g", [128, 4, 512])
    tmp_y = sb("tmp_y", [128, 512])
    stat = sb("stat", [128, 64])
    bnsts = [sb("bnst0", [128, 6]), sb("bnst1", [128, 6])]
    mvs = [sb("mv0", [128, 2]), sb("mv1", [128, 2])]
    zzs = [sb("zz0", [128, 8]), sb("zz1", [128, 8])]
    rzs = [sb("rz0", [128, 8]), sb("rz1", [128, 8])]

    wTs_f = relu_t[0][0:64, :].rearrange("p (g t) -> p g t", t=64)
    dqk = relu_t[1][:, 0:128]
    tmpf = relu_t[1][:, 128:256]
    tmpf2 = relu_t[1][:, 256:384]
    identf = relu_t[1][:, 384:512]
    biasP0 = sb("biasP0", [128, 2, 512], BF16)
    def kTc(hp, j):
        return h2T[hp, j, 128:256]

    def Vc(j):
        return h2T[:, 16 + j, 64:196].rearrange("p (h e) -> p h e", e=66)

    bank = [psum("bank%d" % i, [128, 512]) for i in range(7)]
    pT = psum("pT", [128, 8, 128], BF16)
    B_bank = [Buf(excl=True) for _ in range(7)]
    B_pT = Buf(excl=True)
    MA = [0, 1]
    M2 = 2
    FB = [3, 4, 5, 6]

    B = {}

    def bf(name):
        if name not in B:
            B[name] = Buf()
        return B[name]

    _stat_i = [0]
    stat_cols = {}

    def st(name):
        if name not in stat_cols:
            stat_cols[name] = _stat_i[0]
            _stat_i[0] += 1
            assert _stat_i[0] <= 64
        c = stat_cols[name]
        return stat[:, c:c + 1], bf("stat_" + name)

    B_x = [[Buf() for _ in range(4)] for _ in range(2)]
    S_xload = [[Sem(nc, es, "xl%d%d" % (s, b)) for b in range(4)] for s in range(2)]
    S_xstore = [[Sem(nc, es, "xs%d%d" % (s, b)) for b in range(4)] for s in range(2)]
    for s in range(2):
        out_sems.extend(S_xstore[s])
    B_actT = [Buf() for _ in range(4)]
    B_x2T = [Buf() for _ in range(4)]
    B_qT = [Buf() for _ in range(4)]
    B_kT = [Buf() for _ in range(5)]
    B_V = [Buf() for _ in range(5)]
    B_PT = [Buf(), Buf()]
    B_h2T = [Buf() for _ in range(32)]
    B_cast = [Buf() for _ in range(NPIECE)]
    S_cast = [Sem(nc, es, "cs%d" % i) for i in range(NPIECE)]
    S_setup = Sem(nc, es, "setup")
    _misc_n = [0]

    def new_out_sem():
        _misc_n[0] += 1
        sm = Sem(nc, es, "misc%d" % _misc_n[0])
        out_sems.append(sm)
        return sm

    B_setup = Buf()

    op(POOL, lambda: nc.gpsimd.memset(nhalf[:], -0.5), writes=[bf("nhalf")])
    op(POOL, lambda: nc.gpsimd.memset(zeros[:], 0.0), writes=[bf("zeros")])
    op(POOL, lambda: nc.gpsimd.memset(identf, 1.0), writes=[bf("identf")])
    op(POOL, lambda: nc.gpsimd.affine_select(out=ident[:], in_=identf, pattern=[[-1, 128]],
                                             compare_op=ALU.is_equal, fill=0.0, base=0,
                                             channel_multiplier=1),
       reads=[bf("identf")], writes=[bf("ident")])
    op(POOL, lambda: nc.gpsimd.memset(Vaug[:], 1.0), writes=B_V)
    op(POOL, lambda: nc.gpsimd.memset(Tcache[:], NEG), writes=[bf("Tcache")])
    op(POOL, lambda: nc.gpsimd.memset(biasN[:], 0.0), writes=[bf("biasN")])
    op(POOL, lambda: nc.gpsimd.iota(dqk, pattern=[[1, 128]], base=0, channel_multiplier=-1,
                                    allow_small_or_imprecise_dtypes=True), writes=[bf("dqk")])
    op(POOL, lambda: nc.gpsimd.iota(d4[:], pattern=[[1, 4]], base=128, channel_multiplier=-1,
                                    allow_small_or_imprecise_dtypes=True), writes=[bf("d4")])

    P_U, P_V, P_Q, P_KV, P_O0, P_O1 = 0, 1, 2, 3, 4, 5
    P_UP = [6 + j for j in range(8)]
    P_DN = [[14 + c * 4 + kg for kg in range(4)] for c in range(2)]
    piece_w = {P_KV: 256}

    def kpc(ap):
        return ap.rearrange("(k p) c -> p k c", p=128)

    def cast(pid, src_ap):
        w = piece_w.get(pid, 512)
        dma(POOL, S_cast[pid], scratch[pid, :, :, 0:w], src_ap, writes=[B_cast[pid]])

    cast(P_KV, kpc(w_in[:, 1536:1792]))
    cast(P_U, kpc(w_in[:, 0:512]))
    cast(P_V, kpc(w_in[:, 512:1024]))

    def casts_mixer_rest():
        for hk in range(2):
            for r in range(4):
                c0 = 1024 + hk * 256 + r * 64
                o0 = (r * 2 + hk) * 64
                dma(POOL, S_cast[P_Q], scratch[P_Q, :, :, o0:o0 + 64], kpc(w_in[:, c0:c0 + 64]),
                    writes=[B_cast[P_Q]])
        cast(P_O0, kpc(w_o[:, 0:512]))
        cast(P_O1, kpc(w_o[:, 512:1024]))

    deferred_casts = []
    for j in range(8):
        deferred_casts.append((P_UP[j], kpc(w_up[:, j * 512:(j + 1) * 512])))
    for c in range(2):
        for kg in range(4):
            deferred_casts.append((P_DN[c][kg], kpc(w_down[kg * 1024:(kg + 1) * 1024, c * 512:(c + 1) * 512])))

    class Ring:
        def __init__(self, name, tiles, seq):
            self.tiles = tiles
            self.R = len(tiles)
            self.bufs = [Buf() for _ in tiles]
            self.sems = [Sem(nc, es, "%s%d" % (name, i)) for i in range(self.R)]
            self.seq = seq
            self.next = 0
            self.cons = 0

        def load_next(self):
            n = self.next
            if n >= len(self.seq):
                return
            pid, view, src_ap = self.seq[n]
            assert B_cast[pid].w is not None, ("ring load emitted before its cast", pid)
            s = n % self.R
            dma(SP, self.sems[s], view(self.tiles[s]), src_ap, reads=[B_cast[pid]], writes=[self.bufs[s]])
            self.next = n + 1

        def open(self, pid):
            n = self.cons
            assert self.seq[n][0] == pid, (n, self.seq[n][0], pid)
            assert n < self.next
            self.cons = n + 1
            return self.seq[n][1](self.tiles[n % self.R]), self.bufs[n % self.R]

        def retire(self):
            self.load_next()

    def m_entry(pid):
        w = piece_w.get(pid, 512)
        return (pid, lambda t: t[:, :, 0:w], scratch[pid, :, :, 0:w])

    def up_entry(j, half):
        return (P_UP[j], lambda t: t[:, :].rearrange("p (k c) -> p k c", c=256),
                scratch[P_UP[j], :, :, half * 256:(half + 1) * 256])

    def dn_entry(c, kg, half):
        return (P_DN[c][kg], lambda t: t[:, :].rearrange("p (k c) -> p k c", c=512),
                scratch[P_DN[c][kg], :, half * 4:(half + 1) * 4, :])

    seqM = [m_entry(P_KV)] + [m_entry(p) for p in (P_U, P_V, P_Q, P_KV, P_O0, P_O1)] * (NSB + 1)
    seqF1 = [up_entry(j, h) for j in range(8) for h in range(2)] + \
            [dn_entry(c, kg, h) for c in range(2) for kg in range(4) for h in range(2)]
    seqF = seqF1 * (NSB + 1)
    ringM = Ring("rm", ringM_t, seqM)
    ringF = Ring("rf", ringF_t, seqF)

    def sdma(out, in_, **kw):
        dma(SP, S_setup, out, in_, writes=[B_setup], **kw)

    S_halo = Sem(nc, es, "halo")
    dma(SP, S_halo, t_o, xh[:, :], writes=[bf("tB0_1"), bf("tB0_2")])
    dma(SP, S_xload[0][0], xbuf[0][:64, 0, :], xs[:, :], writes=[B_x[0][0]])
    sdma(gpost_bc[:], g_post_mix.partition_broadcast(128))
    sdma(gpostffn_bc[:], g_post_ffn.partition_broadcast(128))
    sdma(lng_bc[:], ln_v_g.partition_broadcast(128))
    sdma(lnb_bc[:], ln_v_b.partition_broadcast(128))
    sdma(esink[:], attn_sinks.partition_broadcast(128))
    sdma(flag[:], flag_d[:, :])
    identF = tB[1][:, 0:128]
    Gst = tB[1][0:32, 128:256]
    sdma(Gst[0:8, :], g_pre_mix[0, :].rearrange("(k p) -> k p", p=128))
    sdma(Gst[8:16, :], g_pre_ffn[0, :].rearrange("(k p) -> k p", p=128))
    sdma(Gst[16:20, :], g_out_chunk[0, :].rearrange("(k p) -> k p", p=128))
    sdma(Gst[20:24, :], g_out_attn[0, :].rearrange("(k p) -> k p", p=128))
    sdma(Gst[24:32, :], b_spatial[:, :])
    w_stage = tB[1][:, 512:1536]
    dma(SP, S_setup, w_stage.rearrange("p (g s) -> p g s", s=128), w_spatial.rearrange("g t s -> t g s"),
        writes=[B_setup, bf("tB1_1"), bf("tB1_2")])
    S_bts = Sem(nc, es, "bts")
    for j in range(SEQ_CORE):
        dma(SP, S_bts, btok_s[4 * j:4 * j + 4, :], b_spatial[:, 0:4].rearrange("g t -> t g"),
            writes=[bf("btok_s")], allow_slow_non_contiguous=True)
    xb1 = xbuf[1]
    S_cache = [Sem(nc, es, "cache%d" % i) for i in range(4)]
    for half in range(2):
        dma(SP, S_cache[half], xb1[:, half, :].rearrange("p (j c) -> p j c", c=128),
            ck[half * 8:(half + 1) * 8].rearrange("j p c -> p j c"), writes=[B_x[1][half]])
        dma(SP, S_cache[2 + half], xb1[:, 2 + half, :].rearrange("p (j c) -> p j c", c=128),
            cv[half * 8:(half + 1) * 8].rearrange("j p c -> p j c"), writes=[B_x[1][2 + half]])
    for _ in range(RING_M):
        ringM.load_next()

    def setup_barrier(E):
        E.e.wait_ge(S_setup.h, S_setup.total)
        E.waited[S_setup] = S_setup.total

    for E in (POOL, DVE, ACT, PE):
        setup_barrier(E)

    op(ACT, lambda: nc.scalar.activation(out=esink[:], in_=esink[:], func=AF.Exp), writes=[bf("esink")])
    op(POOL, lambda: nc.gpsimd.affine_select(out=identF, in_=identf, pattern=[[-1, 128]],
                                             compare_op=ALU.is_equal, fill=0.0, base=0, channel_multiplier=1),
       reads=[bf("identf")], writes=[bf("tB1_0")])
    op(PE, lambda: nc.tensor.transpose(out=bank[M2][:, 0:32], in_=Gst, identity=identF[0:32, 0:32]),
       reads=[bf("tB1_0")], writes=[B_bank[M2]])
    op(DVE, lambda: nc.vector.tensor_copy(out=GT[:], in_=bank[M2][:, 0:32]), reads=[B_bank[M2]], writes=[bf("GT")])
    op(DVE, lambda: nc.vector.tensor_copy(out=h_bf[:], in_=w_stage), reads=[bf("tB1_1"), bf("tB1_2")], writes=[bf("tA0")])
    for g in range(8):
        op(PE, lambda: nc.tensor.transpose(out=pT[:, g, :], in_=h_bf[:, g * 128:(g + 1) * 128],
                                           identity=ident[:]),
           reads=[bf("tA0"), bf("ident")], writes=[B_pT], signal=(g == 7))
    op(DVE, lambda: nc.vector.tensor_copy(out=junk[:].rearrange("p (g t) -> p g t", t=128), in_=pT[:]),
       reads=[B_pT], writes=[bf("tA1")])
    op(POOL, lambda: nc.gpsimd.affine_select(out=wTm[:], in_=junk[:].rearrange("p (g t) -> p g t", t=128),
                                             pattern=[[0, 8], [1, 128]], compare_op=ALU.is_ge, fill=0.0,
                                             base=0, channel_multiplier=-1),
       reads=[bf("tA1")], writes=[bf("wTm")])
    S_TR, S_SP = 6, 10

    def slack(n):
        for _ in range(n):
            yield

    bank_lock = {}

    def acquire(b, who):
        while bank_lock.get(b) is not None:
            yield
        bank_lock[b] = who

    def release(b):
        bank_lock[b] = None

    def rstd_from(P, ss_list, n, name):
        t_ap, t_b = st(name + "_t")
        r_ap, r_b = st(name + "_r")
        src_ap, src_b = ss_list[0]
        if len(ss_list) == 2:
            s_ap, s_b = st(name + "_s")
            op(POOL, lambda: nc.gpsimd.tensor_tensor(out=s_ap[:P], in0=ss_list[0][0][:P],
                                                     in1=ss_list[1][0][:P], op=ALU.add),
               reads=[ss_list[0][1], ss_list[1][1]], writes=[s_b])
            src_ap, src_b = s_ap, s_b
        op(POOL, lambda: nc.gpsimd.tensor_scalar(out=t_ap[:P], in0=src_ap[:P], scalar1=1.0 / n,
                                                 scalar2=EPS, op0=ALU.mult, op1=ALU.add),
           reads=[src_b], writes=[t_b])
        op(POOL, lambda: nc.gpsimd.tensor_tensor(out=r_ap[:P], in0=t_ap[:P], in1=nhalf[:P], op=ALU.pow),
           reads=[t_b, bf("nhalf")], writes=[r_b])
        return r_ap, r_b

    def transposes_to(P, src_bf, src_buf, gT, dstT, dst_buf, tok0):
        yield from slack(S_TR)
        yield from acquire("pT", 0)
        for k in range(8):
            op(PE, lambda: nc.tensor.transpose(out=pT[:, k, 0:P], in_=src_bf[:, k * 128:(k + 1) * 128],
                                               identity=ident[:P, :P]),
               reads=[src_buf, bf("ident")], writes=[B_pT], signal=(k == 7))
        yield from slack(3)
        op(DVE, lambda: nc.vector.tensor_tensor(
            out=dstT[:, :, tok0:tok0 + P], in0=pT[:, :, 0:P],
            in1=gT[:, 0:8].unsqueeze(2).broadcast_to([128, 8, P]), op=ALU.mult),
           reads=[B_pT, bf("GT")], writes=[dst_buf])
        release("pT")
        yield

    def norm_to(P, x_ap, x_buf, gT, dstT, dst_buf, tok0, name, c=0):
        name = name + "_%d" % c
        hb = tA[c]
        hbb = bf("tA%d" % c)
        ss_ap, ss_b = st(name + "_ss")
        op(ACT, lambda: nc.scalar.activation(out=hb[:P], in_=x_ap, func=AF.Square, accum_out=ss_ap[:P]),
           reads=list(x_buf), writes=[hbb, ss_b])
        r_ap, r_b = rstd_from(P, [(ss_ap, ss_b)], D, name)
        op(DVE, lambda: nc.vector.tensor_scalar(out=hb[:P], in0=x_ap, scalar1=r_ap[:P], scalar2=None,
                                                op0=ALU.mult),
           reads=list(x_buf) + [r_b], writes=[hbb])
        yield from transposes_to(P, hb[:P, :], hbb, gT, dstT, dst_buf, tok0)

    def kv_token_major(P, blk, tok0, Wt, Wb, slot, need_f32=False):
        ps = bank[M2]
        c0 = 0 if need_f32 else 128
        for k in range(8):
            op(PE, lambda: nc.tensor.matmul(ps[:P, c0:256], lhsT=actT[:, k, tok0:tok0 + P],
                                            rhs=Wt[:, k, c0:256], start=(k == 0), stop=(k == 7)),
               reads=[B_actT[blk], Wb], writes=[B_bank[M2]], signal=(k == 7))
        yield
        if need_f32:
            op(ACT, lambda: nc.scalar.activation(out=kvf[:P], in_=ps[:P, 0:256], func=AF.Copy),
               reads=[B_bank[M2]], writes=[bf("kvf")])
        op(DVE, lambda: nc.vector.tensor_copy(
            out=Vaug[:P, slot, :, 0:64], in_=ps[:P, 128:256].rearrange("p (h d) -> p h d", d=64)),
           reads=[B_bank[M2]], writes=[B_V[slot]])
        yield

    def k_feature_major(T, nblk, Wt, Wb, col0, slots):
        ps = bank[M2]
        for k in range(8):
            op(PE, lambda: nc.tensor.matmul(ps[:, 0:T], lhsT=Wt[:, k, 0:128], rhs=actT[:, k, 0:T],
                                            start=(k == 0), stop=(k == 7)),
               reads=B_actT[:nblk] + [Wb], writes=[B_bank[M2]], signal=(k == 7))
            if k == 3:
                yield
        yield
        op(ACT, lambda: nc.scalar.activation(out=kT[:, col0:col0 + T], in_=ps[:, 0:T], func=AF.Copy),
           reads=[B_bank[M2]], writes=[B_kT[s] for s in slots])
        yield

    def halo_gen():
        Wt, Wb = ringM.open(P_KV)
        yield from norm_to(128, t_o, [bf("tB0_1"), bf("tB0_2")], gT_pre, actT, B_actT[0], 0, "n1")
        yield from k_feature_major(128, 1, Wt, Wb, 0, [0])
        yield from kv_token_major(128, 0, 0, Wt, Wb, 0)

    _drive(halo_gen())

    casts_mixer_rest()
    ringM.retire()
    for h in range(8):
        hk, r = h // 4, h % 4
        slope = 2.0 ** (-(h + 1))
        op(POOL, lambda: nc.gpsimd.tensor_scalar(out=tmpf, in0=dqk, scalar1=-8.0 * slope,
                                                 scalar2=0.0, op0=ALU.mult, op1=ALU.add),
           reads=[bf("dqk")], writes=[bf("tmpf")])
        op(POOL, lambda: nc.gpsimd.affine_select(out=biasC[:, hk, r * 128:(r + 1) * 128], in_=tmpf,
                                                 pattern=[[1, 128]], compare_op=ALU.is_ge, fill=NEG,
                                                 base=0, channel_multiplier=-1),
           reads=[bf("tmpf")], writes=[bf("biasC")])
        op(POOL, lambda: nc.gpsimd.tensor_scalar(out=tmpf2, in0=dqk, scalar1=-8.0 * slope,
                                                 scalar2=-1024.0 * slope, op0=ALU.mult, op1=ALU.add),
           reads=[bf("dqk")], writes=[bf("tmpf2")])
        op(POOL, lambda: nc.gpsimd.affine_select(out=biasP[:, hk, r * 128:(r + 1) * 128], in_=tmpf2,
                                                 pattern=[[-1, 128]], compare_op=ALU.is_ge, fill=NEG,
                                                 base=0, channel_multiplier=1),
           reads=[bf("tmpf2")], writes=[bf("biasP")])
        op(POOL, lambda: nc.gpsimd.tensor_scalar(out=tmpf[:, 0:4], in0=d4[:], scalar1=-8.0 * slope,
                                                 scalar2=0.0, op0=ALU.mult, op1=ALU.add),
           reads=[bf("d4")], writes=[bf("tmpf")])
        op(POOL, lambda: nc.gpsimd.affine_select(out=Tcache[:, h, 60:64], in_=tmpf[:, 0:4],
                                                 pattern=[[-1, 4]], compare_op=ALU.is_ge, fill=NEG,
                                                 base=0, channel_multiplier=1),
           reads=[bf("tmpf")], writes=[bf("Tcache")])
        op(POOL, lambda: nc.gpsimd.tensor_scalar(out=tmpf2[:64, 0:64], in0=dqk[:64, 0:64],
                                                 scalar1=-8.0 * slope, scalar2=0.0,
                                                 op0=ALU.mult, op1=ALU.add),
           reads=[bf("dqk")], writes=[bf("tmpf2")])
        op(POOL, lambda: nc.gpsimd.affine_select(out=tmpf[:64, 0:64], in_=tmpf2[:64, 0:64],
                                                 pattern=[[1, 64]], compare_op=ALU.is_ge, fill=NEG,
                                                 base=0, channel_multiplier=-1),
           reads=[bf("tmpf2")], writes=[bf("tmpf")])
        op(POOL, lambda: nc.gpsimd.affine_select(
            out=biasN[:64, h, :].rearrange("p (j t) -> p j t", t=4),
            in_=tmpf[:64, 0:64].rearrange("p (j t) -> p j t", t=4),
            pattern=[[-4, 16], [0, 4]], compare_op=ALU.is_ge, fill=NEG, base=0, channel_multiplier=1),
           reads=[bf("tmpf")], writes=[bf("biasN")])
    op(DVE, lambda: nc.vector.tensor_scalar(out=biasP0[:], in0=biasP[:], scalar1=flag[:, 0:1],
                                            scalar2=None, op0=ALU.add),
       reads=[bf("biasP")], writes=[bf("biasP0")])
    op(DVE, lambda: nc.vector.tensor_copy(out=Rsel[:].rearrange("p (j t) -> p j t", t=4),
                                          in_=ident[0:4, 0:4].unsqueeze(1).broadcast_to([4, 16, 4])),
       reads=[bf("ident")], writes=[bf("Rsel")])
    op(DVE, lambda: nc.vector.tensor_copy(out=wsm[:].rearrange("p (g t) -> p g t", t=4), in_=wTm[0:4, :, 0:4]),
       reads=[bf("wTm")], writes=[bf("wsm")])
    op(PE, lambda: nc.tensor.matmul(bank[M2][:64, 0:32], lhsT=Rsel[:, :], rhs=wsm[:, :], start=True, stop=True),
       reads=[bf("Rsel"), bf("wsm")], writes=[B_bank[M2]])
    wTs_f4 = wTs_f.rearrange("p g (j t) -> p g j t", t=4)
    op(DVE, lambda: nc.vector.tensor_copy(
        out=wTs_f4,
        in_=bank[M2][:64, 0:32].rearrange("p (g t) -> p g t", t=4).unsqueeze(2).broadcast_to([64, 8, 16, 4])),
       reads=[B_bank[M2]], writes=[bf("wTs_f")])
    op(POOL, lambda: nc.gpsimd.affine_select(out=wTs_f4, in_=wTs_f4,
                                             pattern=[[0, 8], [-4, 16], [0, 4]], compare_op=ALU.is_ge,
                                             fill=0.0, base=0, channel_multiplier=1),
       reads=[bf("wTs_f")], writes=[bf("wTs_f")])
    op(POOL, lambda: nc.gpsimd.affine_select(out=wTs[:].rearrange("p g (j t) -> p g j t", t=4), in_=wTs_f4,
                                             pattern=[[0, 8], [4, 16], [0, 4]], compare_op=ALU.is_ge,
                                             fill=0.0, base=3, channel_multiplier=-1),
       reads=[bf("wTs_f")], writes=[bf("wTs"), bf("relu0"), bf("relu1")])


    def rr(*gens):
        active = list(gens)
        while active:
            for g in list(active):
                try:
                    next(g)
                except StopIteration:
                    active.remove(g)
            yield

    def mixer_gen(sbi, P, nblk, sample):
        T = P * nblk
        pos = 0 if sample else sbi + 1
        xs_i = pos % 2
        xb = xbuf[xs_i]
        Bx = B_x[xs_i]
        last_prompt = (not sample) and sbi == NSB - 1

        def chainA(c):
            for b in range(c, nblk, 2):
                yield from norm_to(P, xb[:P, b, :], [Bx[b]], gT_pre, actT, B_actT[b], b * P, "n1", c)

        yield from rr(chainA(0), chainA(1))

        Wu, Wub = ringM.open(P_U)
        Wv, Wvb = ringM.open(P_V)

        def chainB(c):
            u_t = tB[c][:, 0:512]
            v_t = tB[c][:, 512:1024]
            a_t = tB[c][:, 1024:1536]
            Bu, Bv, Ba = bf("tB%d_0" % c), bf("tB%d_1" % c), bf("tB%d_2" % c)
            v_bf = v_bfs[c]
            Bvb = bf("v_bf%d" % c)
            bnst, mv = bnsts[c], mvs[c]
            Bbn, Bmv = bf("bnst%d" % c), bf("mv%d" % c)
            for b in range(c, nblk, 2):
                tk = slice(b * P, (b + 1) * P)
                for n, (Wt, Wb) in enumerate(((Wu, Wub), (Wv, Wvb))):
                    yield from acquire(MA[n], c)
                    for k in range(8):
                        op(PE, lambda: nc.tensor.matmul(bank[MA[n]][:P, :], lhsT=actT[:, k, tk], rhs=Wt[:, k, :],
                                                        start=(k == 0), stop=(k == 7)),
                           reads=[B_actT[b], Wb], writes=[B_bank[MA[n]]], signal=(k == 7))
                    yield from slack(4)
                    dst_t, dst_b = (u_t, Bu) if n == 0 else (v_t, Bv)
                    op(ACT, lambda: nc.scalar.activation(out=dst_t[:P], in_=bank[MA[n]][:P, :], func=AF.Gelu_apprx_tanh),
                       reads=[B_bank[MA[n]]], writes=[dst_b])
                    release(MA[n])
                    yield
                op(DVE, lambda: nc.vector.bn_stats(out=bnst[:P], in_=v_t[:P]), reads=[Bv], writes=[Bbn])
                op(DVE, lambda: nc.vector.bn_aggr(out=mv[:P], in_=bnst[:P]), reads=[Bbn], writes=[Bmv])
                yield
                rv_ap, rv_b = rstd_from(P, [(mv[:, 1:2], Bmv)], 1.0, "lnv%d" % c)
                yield
                op(DVE, lambda: nc.vector.tensor_scalar(out=v_t[:P], in0=v_t[:P], scalar1=mv[:P, 0:1],
                                                        scalar2=rv_ap[:P], op0=ALU.subtract, op1=ALU.mult),
                   reads=[Bv, Bmv, rv_b], writes=[Bv])
                yield
                op(POOL, lambda: nc.gpsimd.tensor_tensor(out=v_t[:P], in0=v_t[:P], in1=lng_bc[:P], op=ALU.mult),
                   reads=[Bv], writes=[Bv])
                yield
                need_f32 = sample or (last_prompt and b == nblk - 1)
                if need_f32:
                    op(POOL, lambda: nc.gpsimd.tensor_tensor(out=v_t[:P], in0=v_t[:P], in1=lnb_bc[:P], op=ALU.add),
                       reads=[Bv], writes=[Bv])
                    yield
                    op(DVE, lambda: nc.vector.tensor_copy(out=v_bf[:P], in_=v_t[:P]), reads=[Bv], writes=[Bvb])
                    if sample:
                        dma(SP, new_out_sem(), cvs[:, :], v_t[:P], reads=[Bv])
                    else:
                        dma(SP, new_out_sem(), cvp[:, :], v_t[:P], reads=[Bv])
                else:
                    op(POOL, lambda: nc.gpsimd.tensor_tensor(out=v_bf[:P], in0=v_t[:P], in1=lnb_bc[:P], op=ALU.add),
                       reads=[Bv], writes=[Bvb])
                yield
                wsp = wTs if sample else wTm
                wspb = bf("wTs") if sample else bf("wTm")
                yield from slack(S_SP)
                yield from acquire(M2, c)
                for g in range(8):
                    op(PE, lambda: nc.tensor.matmul(bank[M2][:P, g * 64:(g + 1) * 64], lhsT=wsp[:P, g, 0:P],
                                                    rhs=v_bf[:P, g * 64:(g + 1) * 64], start=True, stop=True),
                       reads=[Bvb, wspb], writes=[B_bank[M2]], signal=(g == 7))
                yield from slack(3)
                bt = btok_s if sample else btok
                op(DVE, lambda: nc.vector.tensor_tensor(
                    out=a_t[:P].rearrange("p (g d) -> p g d", d=64),
                    in0=bank[M2][:P, :].rearrange("p (g d) -> p g d", d=64),
                    in1=bt[:P, 0:8].unsqueeze(2).broadcast_to([P, 8, 64]), op=ALU.add),
                   reads=[B_bank[M2], bf("GT"), bf("btok_s")], writes=[Ba])
                release(M2)
                yield
                op(POOL, lambda: nc.gpsimd.tensor_tensor(out=a_t[:P], in0=a_t[:P], in1=u_t[:P], op=ALU.mult),
                   reads=[Ba, Bu], writes=[Ba])
                yield
                ssa_ap, ssa_b = st("ssa%d" % c)
                op(ACT, lambda: nc.scalar.activation(out=mrg4[:P, b, 0:512], in_=a_t[:P], func=AF.Square,
                                                     accum_out=ssa_ap[:P]),
                   reads=[Ba], writes=[bf("mrg%d" % b), ssa_b])
                yield
                ra_ap, ra_b = rstd_from(P, [(ssa_ap, ssa_b)], 512, "rsa%d" % c)
                yield
                op(DVE, lambda: nc.vector.tensor_scalar(out=mrg4[:P, b, 0:512], in0=a_t[:P], scalar1=ra_ap[:P],
                                                        scalar2=None, op0=ALU.mult),
                   reads=[Ba, ra_b], writes=[bf("mrg%d" % b)])
                yield

        yield from rr(chainB(0), chainB(1))
        ringM.retire()
        ringM.retire()

        Wq, Wqb = ringM.open(P_Q)
        for c in range(4):
            pb = MA[c % 2]
            for k in range(8):
                op(PE, lambda: nc.tensor.matmul(bank[pb][:, 0:T], lhsT=Wq[:, k, c * 128:(c + 1) * 128],
                                                rhs=actT[:, k, 0:T], start=(k == 0), stop=(k == 7)),
                   reads=B_actT[:nblk] + [Wqb], writes=[B_bank[pb]], signal=(k == 7))
                if k == 3 or k == 7:
                    yield
            if c % 2 == 0:
                op(ACT, lambda: nc.scalar.activation(out=qT[:, c, 0:T], in_=bank[pb][:, 0:T], func=AF.Copy),
                   reads=[B_bank[pb]], writes=[B_qT[c]])
            else:
                op(DVE, lambda: nc.vector.tensor_copy(out=qT[:, c, 0:T], in_=bank[pb][:, 0:T]),
                   reads=[B_bank[pb]], writes=[B_qT[c]])
        ringM.retire()
        Wkv, Wkvb = ringM.open(P_KV)
        yield from k_feature_major(T, nblk, Wkv, Wkvb, 128, list(range(1, 1 + nblk)))
        for b in range(nblk):
            yield from kv_token_major(P, b, b * P, Wkv, Wkvb, 1 + b,
                                      need_f32=(sample or (last_prompt and b == nblk - 1)))
            if sample:
                dma(SP, new_out_sem(), wks[:, 124:128, :], kvf[:P, 0:128], reads=[bf("kvf")])
                dma(SP, new_out_sem(), wvs[:, 124:128, :], kvf[:P, 128:256], reads=[bf("kvf")])
            elif last_prompt and b == nblk - 1:
                dma(SP, new_out_sem(), wkp[:, :], kvf[:P, 0:128], reads=[bf("kvf")])
                dma(SP, new_out_sem(), wvp[:, :], kvf[:P, 128:256], reads=[bf("kvf")])
        ringM.retire()

        def norm_attn(c, b, b_t, Bb, zz, rz, Bzz, Brz, pbanks):
            for hk in range(2):
                pb = pbanks[hk]
                pov = bank[pb][:P, 0:260].rearrange("p (r e) -> p r e", e=65)
                op(DVE, lambda: nc.vector.tensor_tensor(out=zz[:P, hk * 4:(hk + 1) * 4], in0=pov[:, :, 64],
                                                        in1=esink[:P, hk * 4:(hk + 1) * 4], op=ALU.add),
                   reads=[B_bank[pb], bf("esink")], writes=[Bzz])
                op(DVE, lambda: nc.vector.reciprocal(out=rz[:P, hk * 4:(hk + 1) * 4], in_=zz[:P, hk * 4:(hk + 1) * 4]),
                   reads=[Bzz], writes=[Brz])
                op(DVE, lambda: nc.vector.tensor_tensor(
                    out=b_t[:P, hk * 256:(hk + 1) * 256].rearrange("p (r d) -> p r d", d=64),
                    in0=pov[:, :, 0:64],
                    in1=rz[:P, hk * 4:(hk + 1) * 4].unsqueeze(2).broadcast_to([P, 4, 64]), op=ALU.mult),
                   reads=[B_bank[pb], Brz], writes=[Bb])

        def chainD(c):
            b_t = tB[c][:, 0:512]
            Bb = bf("tB%d_0" % c)
            zz, rz = zzs[c], rzs[c]
            Bzz, Brz = bf("zz%d" % c), bf("rz%d" % c)
            for b in range(c, nblk, 2):
                if not sample:
                    PTt = PTb[b % 2]
                    BPT = B_PT[b % 2]
                    qc = slice(b * 128, (b + 1) * 128)
                    for hk in range(2):
                        hp = slice(hk * 64, (hk + 1) * 64)
                        for kb in range(2):
                            pb = MA[kb]
                            if kb == 0:
                                bias_t, bias_b = (biasP0, bf("biasP0")) if b == 0 and sbi == 0 else (biasP, bf("biasP"))
                            else:
                                bias_t, bias_b = biasC, bf("biasC")
                            kc = slice((b + kb) * 128, (b + kb + 1) * 128)
                            yield from acquire(pb, c)
                            op(PE, lambda: nc.tensor.matmul(bank[pb][:, :], lhsT=ident[:], rhs=bias_t[:, hk, :],
                                                            start=True, stop=False, skip_group_check=True),
                               reads=[bias_b, bf("ident")], writes=[B_bank[pb]], signal=False)
                            for r in range(4):
                                op(PE, lambda: nc.tensor.matmul(bank[pb][:, r * 128:(r + 1) * 128], lhsT=kT[hp, kc],
                                                                rhs=qT[hp, r, qc], start=False, stop=(r == 3),
                                                                skip_group_check=True),
                                   reads=[B_kT[b + kb], B_qT[r]], writes=[B_bank[pb]], signal=(r == 3))
                            yield from slack(2)
                            op(ACT, lambda: nc.scalar.activation(out=PTt[:, hk * 2 + kb, :], in_=bank[pb][:, :],
                                                                 func=AF.Exp, scale=0.125),
                               reads=[B_bank[pb]], writes=[BPT])
                            release(pb)
                            yield
                    for hk in range(2):
                        yield from acquire(M2, c)
                        for r in range(4):
                            for kb in range(2):
                                op(PE, lambda: nc.tensor.matmul(bank[M2][:, r * 65:(r + 1) * 65],
                                                                lhsT=PTt[:, hk * 2 + kb, r * 128:(r + 1) * 128],
                                                                rhs=Vaug[:, b + kb, hk, 0:65], start=(kb == 0), stop=(kb == 1)),
                                   reads=[BPT, B_V[b + kb]], writes=[B_bank[M2]], signal=(kb == 1 and r == 3))
                        yield from slack(2)
                        pov = bank[M2][:P, 0:260].rearrange("p (r e) -> p r e", e=65)
                        op(DVE, lambda: nc.vector.tensor_tensor(out=zz[:P, hk * 4:(hk + 1) * 4], in0=pov[:, :, 64],
                                                                in1=esink[:P, hk * 4:(hk + 1) * 4], op=ALU.add),
                           reads=[B_bank[M2], bf("esink")], writes=[Bzz])
                        op(DVE, lambda: nc.vector.reciprocal(out=rz[:P, hk * 4:(hk + 1) * 4], in_=zz[:P, hk * 4:(hk + 1) * 4]),
                           reads=[Bzz], writes=[Brz])
                        op(DVE, lambda: nc.vector.tensor_tensor(
                            out=b_t[:P, hk * 256:(hk + 1) * 256].rearrange("p (r d) -> p r d", d=64),
                            in0=pov[:, :, 0:64],
                            in1=rz[:P, hk * 4:(hk + 1) * 4].unsqueeze(2).broadcast_to([P, 4, 64]), op=ALU.mult),
                           reads=[B_bank[M2], Brz], writes=[Bb])
                        release(M2)
                        yield
                else:
                    POb = [M2, FB[0]]
                    for hk in range(2):
                        op(PE, lambda: nc.tensor.matmul(bank[POb[hk]][:P, 0:260], lhsT=zeros[:, 0:P], rhs=zeros[:, 0:260],
                                                        start=True, stop=False, skip_group_check=True),
                           reads=[bf("zeros")], writes=[B_bank[POb[hk]]], signal=False)
                    for j in range(SEQ_CORE + 1):
                        new = (j == SEQ_CORE)
                        KP = 64 if new else 128
                        PTt = PTb[j % 2]
                        BPT = B_PT[j % 2]
                        if new:
                            rhs_bias = biasN[:, :, :].rearrange("p h q -> p (h q)")
                            bias_b = bf("biasN")
                        else:
                            bj = biasJ[j % 2]
                            bias_b = bf("biasJ%d" % (j % 2))
                            op(POOL, lambda: nc.gpsimd.tensor_copy(out=bj[:].rearrange("p (h q) -> p h q", q=64),
                                                                   in_=Tcache[:, :, 60 - 4 * j:124 - 4 * j]),
                               reads=[bf("Tcache")], writes=[bias_b])
                            rhs_bias = bj[:, :]
                        for hk in range(2):
                            pb = MA[hk]
                            hp = slice(hk * 64, (hk + 1) * 64)
                            op(PE, lambda: nc.tensor.matmul(bank[pb][:KP, 0:256], lhsT=ident[:, :KP],
                                                            rhs=rhs_bias[:, hk * 256:(hk + 1) * 256],
                                                            start=True, stop=False, skip_group_check=True),
                               reads=[bias_b, bf("ident")], writes=[B_bank[pb]], signal=False)
                            for r in range(4):
                                lhs = kT[hp, 128:128 + 64] if new else kTc(hp, j)
                                op(PE, lambda: nc.tensor.matmul(bank[pb][:KP, r * 64:(r + 1) * 64], lhsT=lhs,
                                                                rhs=qT[hp, r, 0:64], start=False, stop=(r == 3),
                                                                skip_group_check=True),
                                   reads=[B_kT[1], bf("kTc"), B_qT[r]], writes=[B_bank[pb]], signal=(r == 3))
                            op(ACT, lambda: nc.scalar.activation(out=PTt[:KP, 0, hk * 256:(hk + 1) * 256],
                                                                 in_=bank[pb][:KP, 0:256], func=AF.Exp, scale=0.125),
                               reads=[B_bank[pb]], writes=[BPT])
                        for h in range(8):
                            hk, r = h // 4, h % 4
                            rhs = Vaug[:KP, 1, hk, 0:65] if new else Vc(j)[:, hk, 0:65]
                            op(PE, lambda: nc.tensor.matmul(bank[POb[hk]][:P, r * 65:(r + 1) * 65],
                                                            lhsT=PTt[:KP, 0, h * 64:(h + 1) * 64], rhs=rhs,
                                                            start=False, stop=new, skip_group_check=True),
                               reads=[BPT, B_V[1], bf("Vc")], writes=[B_bank[POb[hk]]], signal=(h == 3 or h == 7))
                        yield
                    norm_attn(c, b, b_t, Bb, zz, rz, Bzz, Brz, POb)
                ssb_ap, ssb_b = st("ssb%d" % c)
                op(ACT, lambda: nc.scalar.activation(out=mrg4[:P, b, 512:1024], in_=b_t[:P], func=AF.Square,
                                                     accum_out=ssb_ap[:P]),
                   reads=[Bb], writes=[bf("mrg%d" % b), ssb_b])
                yield
                rb_ap, rb_b = rstd_from(P, [(ssb_ap, ssb_b)], 512, "rsb%d" % c)
                yield
                op(DVE, lambda: nc.vector.tensor_scalar(out=mrg4[:P, b, 512:1024], in0=b_t[:P], scalar1=rb_ap[:P],
                                                        scalar2=None, op0=ALU.mult),
                   reads=[Bb, rb_b], writes=[bf("mrg%d" % b)])
                yield
                yield from transposes_to(P, mrg4[:P, b, :], bf("mrg%d" % b), gT_mrg, actT, B_actT[b], b * P)

        yield from rr(chainD(0), chainD(1))
        if not sample:
            op(POOL, lambda: nc.gpsimd.tensor_copy(out=kT[:, 0:128], in_=kT[:, 512:640]),
               reads=[B_kT[4]], writes=[B_kT[0]])
            op(POOL, lambda: nc.gpsimd.tensor_copy(out=Vaug[:, 0, :, :], in_=Vaug[:, 4, :, :]),
               reads=[B_V[4]], writes=[B_V[0]])

        Wo0, Wo0b = ringM.open(P_O0)
        Wo1, Wo1b = ringM.open(P_O1)

        def chainE(c):
            t_oc = tB[c][:, 512:1536]
            Bto = [bf("tB%d_1" % c), bf("tB%d_2" % c)]
            hb = tA[c]
            hbb = bf("tA%d" % c)
            for b in range(c, nblk, 2):
                tk = slice(b * P, (b + 1) * P)
                ss = []
                for n, (Wt, Wb) in enumerate(((Wo0, Wo0b), (Wo1, Wo1b))):
                    yield from acquire(MA[n], c)
                    for k in range(8):
                        op(PE, lambda: nc.tensor.matmul(bank[MA[n]][:P, :], lhsT=actT[:, k, tk], rhs=Wt[:, k, :],
                                                        start=(k == 0), stop=(k == 7)),
                           reads=[B_actT[b], Wb], writes=[B_bank[MA[n]]], signal=(k == 7))
                    yield from slack(4)
                    s_ap, s_b = st("sso%d_%d" % (n, c))
                    op(ACT, lambda: nc.scalar.activation(out=hb[:P, n * 512:(n + 1) * 512], in_=bank[MA[n]][:P, :],
                                                         func=AF.Square, accum_out=s_ap[:P]),
                       reads=[B_bank[MA[n]]], writes=[hbb, s_b])
                    ss.append((s_ap, s_b))
                    op(DVE, lambda: nc.vector.tensor_tensor(out=t_oc[:P, n * 512:(n + 1) * 512], in0=bank[MA[n]][:P, :],
                                                            in1=gpost_bc[:P, n * 512:(n + 1) * 512], op=ALU.mult),
                       reads=[B_bank[MA[n]]], writes=[Bto[n]])
                    release(MA[n])
                    yield
                ro_ap, ro_b = rstd_from(P, ss, D, "rso%d" % c)
                yield
                op(DVE, lambda: nc.vector.scalar_tensor_tensor(out=xb[:P, b, :], in0=t_oc[:P], scalar=ro_ap[:P],
                                                               in1=xb[:P, b, :], op0=ALU.mult, op1=ALU.add),
                   reads=Bto + [ro_b, Bx[b]], writes=[Bx[b]])
                yield

        yield from rr(chainE(0), chainE(1))
        ringM.retire()
        ringM.retire()

        def chainF(c):
            for b in range(c, nblk, 2):
                yield from norm_to(P, xb[:P, b, :], [Bx[b]], gT_ffn, x2T, B_x2T[b], b * P, "n2", c)

        yield from rr(chainF(0), chainF(1))

    def ffn_gen(sbi, P, nblk, sample):
        T = P * nblk
        pos = 0 if sample else sbi + 1
        xs_i = pos % 2
        xb = xbuf[xs_i]
        Bx = B_x[xs_i]
        row0 = 0 if sample else sbi * 512
        def up_evac(f):
            pb = FB[f % 4]
            rt = relu_t[f % 2]
            rb_ = bf("relu%d" % (f % 2))
            op(ACT, lambda: nc.scalar.activation(out=rt[:, 0:T], in_=bank[pb][:, 0:T], func=AF.Relu),
               reads=[B_bank[pb]], writes=[rb_])
            op(DVE, lambda: nc.vector.tensor_tensor(out=h2T[:, f, 0:T], in0=rt[:, 0:T], in1=rt[:, 0:T],
                                                    op=ALU.mult),
               reads=[rb_], writes=[B_h2T[f]])

        pending = None
        for j in range(8):
            for half in range(2):
                Wt, Wb = ringF.open(P_UP[j])
                for fc in range(2):
                    f = j * 4 + half * 2 + fc
                    pb = FB[f % 4]
                    for k in range(8):
                        op(PE, lambda: nc.tensor.matmul(bank[pb][:, 0:T], lhsT=Wt[:, k, fc * 128:(fc + 1) * 128],
                                                        rhs=x2T[:, k, 0:T], start=(k == 0), stop=(k == 7)),
                           reads=B_x2T[:nblk] + [Wb], writes=[B_bank[pb]], signal=(k == 7))
                        if k in (1, 3, 5):
                            yield
                    if pending is not None:
                        up_evac(pending)
                    pending = f
                    yield
                ringF.retire()
        up_evac(pending)
        ssf = [[None, None] for _ in range(nblk)]
        for c in range(2):
            for kg in range(4):
                for half in range(2):
                    Wt, Wb = ringF.open(P_DN[c][kg])
                    for b in range(nblk):
                        tk = slice(b * P, (b + 1) * P)
                        pb = FB[b]
                        for kk in range(4):
                            f = kg * 8 + half * 4 + kk
                            first = (kg == 0 and half == 0 and kk == 0)
                            last = (kg == 3 and half == 1 and kk == 3)
                            op(PE, lambda: nc.tensor.matmul(bank[pb][:P, :], lhsT=h2T[:, f, tk], rhs=Wt[:, kk, :],
                                                            start=first, stop=last),
                               reads=[B_h2T[f], Wb], writes=[B_bank[pb]], signal=(kk == 3))
                            if kk == 1:
                                yield
                        yield
                    ringF.retire()
            for b in range(nblk):
                pb = FB[b]
                s_ap, s_b = st("ssf%d_%d" % (b, c))
                op(ACT, lambda: nc.scalar.activation(out=fsink[:P, :], in_=bank[pb][:P, :], func=AF.Square,
                                                     accum_out=s_ap[:P]),
                   reads=[B_bank[pb]], writes=[bf("fsink"), s_b])
                ssf[b][c] = (s_ap, s_b)
                if c == 0:
                    op(DVE, lambda: nc.vector.tensor_tensor(out=f0g[:P, b, :], in0=bank[pb][:P, :],
                                                            in1=gpostffn_bc[:P, 0:512], op=ALU.mult),
                       reads=[B_bank[pb]], writes=[bf("f0g%d" % b)])
                else:
                    rf_ap, rf_b = rstd_from(P, ssf[b], D, "rsf")
                    op(DVE, lambda: nc.vector.tensor_tensor(out=tmp_y[:P], in0=bank[pb][:P, :],
                                                            in1=gpostffn_bc[:P, 512:1024], op=ALU.mult),
                       reads=[B_bank[pb]], writes=[bf("tmp_y")])
                    op(DVE, lambda: nc.vector.scalar_tensor_tensor(out=xb[:P, b, 512:1024], in0=tmp_y[:P],
                                                                   scalar=rf_ap[:P], in1=xb[:P, b, 512:1024],
                                                                   op0=ALU.mult, op1=ALU.add),
                       reads=[bf("tmp_y"), rf_b, Bx[b]], writes=[Bx[b]])
                    op(DVE, lambda: nc.vector.scalar_tensor_tensor(out=xb[:P, b, 0:512], in0=f0g[:P, b, :],
                                                                   scalar=rf_ap[:P], in1=xb[:P, b, 0:512],
                                                                   op0=ALU.mult, op1=ALU.add),
                       reads=[bf("f0g%d" % b), rf_b, Bx[b]], writes=[Bx[b]])
                    dst = ys if sample else yp
                    dma(SP, S_xstore[xs_i][b], dst[row0 + b * P:row0 + (b + 1) * P, :], xb[:P, b, :],
                        reads=[Bx[b]])
                    if pos + 1 < NSB and not sample:
                        nx = pos + 1
                        lbs = [b - 1] if b >= 1 else []
                        if b == nblk - 1:
                            lbs.append(b)
                        for lb in lbs:
                            dma(SP, S_xload[xs_i][lb], xb[:, lb, :],
                                xp[nx * 512 + lb * 128:nx * 512 + (lb + 1) * 128, :], writes=[Bx[lb]])
                yield
        nxt = pos + 1
        if nxt < NSB and sample:
            for b in range(4):
                dma(SP, S_xload[xs_i][b], xb[:, b, :], xp[nxt * 512 + b * 128:nxt * 512 + (b + 1) * 128, :],
                    writes=[Bx[b]])

    dma(SP, new_out_sem(), wks[:, 0:124, :], ck[:, 4:128, :])
    dma(SP, new_out_sem(), wvs[:, 0:124, :], cv[:, 4:128, :])
    op(POOL, lambda: nc.gpsimd.memset(h2T[:, 16:32, 64:196], 1.0), writes=[bf("Vc")] + B_h2T[16:32])
    for half in range(2):
        ckb = PTb[1][:, half * 2:(half + 1) * 2, :]
        op(DVE, lambda: nc.vector.tensor_copy(out=ckb, in_=xb1[:, half, :].rearrange("p (a c) -> p a c", c=512)),
           reads=[B_x[1][half]], writes=[B_PT[1]])
        for jj in range(8):
            srcap = PTb[1][:, half * 2 + jj // 4, (jj % 4) * 128:(jj % 4 + 1) * 128]
            op(PE, lambda: nc.tensor.transpose(out=pT[:, jj, :], in_=srcap, identity=ident[:]),
               reads=[B_PT[1], bf("ident")], writes=[B_pT], signal=(jj == 7))
        op(DVE, lambda: nc.vector.tensor_copy(out=h2T[:, half * 8:(half + 1) * 8, 128:256], in_=pT[:]),
           reads=[B_pT], writes=[bf("kTc")] + B_h2T[half * 8:(half + 1) * 8])
        op(DVE, lambda: nc.vector.tensor_copy(
            out=h2T[:, 16 + half * 8:16 + (half + 1) * 8, 64:196].rearrange("p j (h e) -> p j h e", e=66)[:, :, :, 0:64],
            in_=xb1[:, 2 + half, :].rearrange("p (j h d) -> p j h d", h=2, d=64)),
           reads=[B_x[1][2 + half], bf("Vc")], writes=[bf("Vc")])
    for b in range(4):
        dma(SP, S_xload[1][b], xb1[:, b, :], xp[b * 128:(b + 1) * 128, :], writes=[B_x[1][b]])

    def prologue_side():
        n = 0
        for pid, src_ap in deferred_casts:
            cast(pid, src_ap)
            n += 1
            if n == 4:
                for _ in range(RING_F):
                    ringF.load_next()
            yield

    _drive(mixer_gen(NSB, 64, 1, True), prologue_side(), ratio=0.25)
    _drive(ffn_gen(NSB, 64, 1, True), mixer_gen(0, 128, 4, False), ratio=1.1)
    for sbi in range(NSB):
        nxt = mixer_gen(sbi + 1, 128, 4, False) if sbi + 1 < NSB else None
        _drive(ffn_gen(sbi, 128, 4, False), nxt, ratio=1.0, delay=6)

    for s in out_sems:
        if s.total > 0:
            SP.e.wait_ge(s.h, s.total)


_CACHE = {}


def kernel(**inputs):
    x_prompt = np.asarray(inputs["x_prompt"], dtype=np.float32)
    x_sample = np.asarray(inputs["x_sample"], dtype=np.float32)
    cache_k = np.asarray(inputs["cache_win_k"], dtype=np.float32)
    cache_v = np.asarray(inputs["cache_win_v"], dtype=np.float32)

    def w(name, shape):
        return np.ascontiguousarray(np.asarray(inputs[name], dtype=np.float32).reshape(shape))

    shared = {
        "w_in": w("w_in", (D, 1792)),
        "g_pre_mix": w("g_pre_mix", (1, D)),
        "ln_v_g": w("ln_v_g", (1, 512)),
        "ln_v_b": w("ln_v_b", (1, 512)),
        "w_spatial": w("w_spatial", (8, 128, 128)),
        "b_spatial": w("b_spatial", (8, 128)),
        "attn_sinks": w("attn_sinks", (1, 8)),
        "g_out_chunk": w("g_out_chunk", (1, 512)),
        "g_out_attn": w("g_out_attn", (1, 512)),
        "w_o": w("w_o", (D, D)),
        "g_post_mix": w("g_post_mix", (1, D)),
        "g_pre_ffn": w("g_pre_ffn", (1, D)),
        "w_up": w("w_up", (D, 4096)),
        "w_down": w("w_down", (4096, D)),
        "g_post_ffn": w("g_post_ffn", (1, D)),
    }
    in_maps = []
    for c in range(NCORES):
        b, half = c // 2, c % 2
        m = dict(shared)
        m["xp"] = np.ascontiguousarray(x_prompt[b, half * TOK_CORE:(half + 1) * TOK_CORE])
        if half == 1:
            m["xh"] = np.ascontiguousarray(x_prompt[b, TOK_CORE - 128:TOK_CORE])
            m["flag"] = np.zeros((128, 1), np.float32)
        else:
            m["xh"] = np.zeros((128, D), np.float32)
            m["flag"] = np.full((128, 1), NEG, np.float32)
        m["xs"] = np.ascontiguousarray(x_sample[c * SEQ_CORE:(c + 1) * SEQ_CORE].reshape(64, D))
        m["ck"] = np.ascontiguousarray(cache_k[0, c * SEQ_CORE:(c + 1) * SEQ_CORE].reshape(SEQ_CORE, 128, 128))
        m["cv"] = np.ascontiguousarray(cache_v[0, c * SEQ_CORE:(c + 1) * SEQ_CORE].reshape(SEQ_CORE, 128, 128))
        in_maps.append(m)

    if "nc" not in _CACHE:
        _CACHE["nc"] = build_program()
    nc, _es = _CACHE["nc"]
    res = run_bass_kernel_spmd(nc, in_maps, core_ids=list(range(NCORES)))
    R = res.results

    y_prompt = np.empty((4, 8192, D), np.float32)
    y_sample = np.empty((128, 4, D), np.float32)
    wk_p = np.empty((1, 4, 128, 2, 64), np.float32)
    wv_p = np.empty((1, 4, 128, 2, 64), np.float32)
    cv_p = np.empty((1, 4, 128, 512), np.float32)
    wk_s = np.empty((1, 128, 128, 2, 64), np.float32)
    wv_s = np.empty((1, 128, 128, 2, 64), np.float32)
    cv_s = np.empty((1, 128, 4, 512), np.float32)
    for c in range(NCORES):
        b, half = c // 2, c % 2
        r = R[c]
        y_prompt[b, half * TOK_CORE:(half + 1) * TOK_CORE] = r["yp"]
        y_sample[c * SEQ_CORE:(c + 1) * SEQ_CORE] = np.asarray(r["ys"]).reshape(SEQ_CORE, 4, D)
        if half == 1:
            wk_p[0, b] = np.asarray(r["wkp"]).reshape(128, 2, 64)
            wv_p[0, b] = np.asarray(r["wvp"]).reshape(128, 2, 64)
            cv_p[0, b] = r["cvp"]
        wk_s[0, c * SEQ_CORE:(c + 1) * SEQ_CORE] = np.asarray(r["wks"]).reshape(SEQ_CORE, 128, 2, 64)
        wv_s[0, c * SEQ_CORE:(c + 1) * SEQ_CORE] = np.asarray(r["wvs"]).reshape(SEQ_CORE, 128, 2, 64)
        cv_s[0, c * SEQ_CORE:(c + 1) * SEQ_CORE] = np.asarray(r["cvs"]).reshape(SEQ_CORE, 4, 512)
    return (y_prompt, y_sample, wk_p, wv_p, cv_p, wk_s, wv_s, cv_s)
```
